# Optimizing a Trainium2 kernel written in Bass

```python
import math
import jax, jax.numpy as jnp
from jax import lax
import numpy as np

D_MODEL = 1024
BATCH = 16
SEQ = 2048
DEPTH = 4

N_MIXERS = 2
N_A_LAYERS = (DEPTH + N_MIXERS - 1) // N_MIXERS
N_B_LAYERS = DEPTH // N_MIXERS
A_HEADS = D_MODEL // 128
A_DHEAD = D_MODEL // (2 * A_HEADS)
A_VDIM = 2 * A_DHEAD
Q_BLOCK = 128
R_HEADS = max(4, D_MODEL // 256)
R_DK = D_MODEL // R_HEADS
R_DV = 2 * R_DK
CHUNK = 128
D_FF = 4 * D_MODEL
PLE_DIM = 256
EPS = 1e-6

kernel_name = "hybrid_diffattn_retnet_sqrelu_ple"


def rms_norm(x, g):
    xf = x.astype(jnp.float32)
    y = xf * lax.rsqrt(jnp.mean(xf * xf, axis=-1, keepdims=True) + EPS)
    return (y * g.astype(jnp.float32)).astype(x.dtype)


def alibi_slopes(n):
    return jnp.array([2.0 ** (-8.0 * (h + 1) / n) for h in range(n)], dtype=jnp.float32)


def diff_attention(xn, w_qkv, w_o, g_q, g_k, lam_q1, lam_k1, lam_q2, lam_k2, g_sub, lambda_init):
    B, S, _ = xn.shape
    qkv = xn @ w_qkv
    q = qkv[..., :D_MODEL].reshape(B, S, A_HEADS, 2, A_DHEAD)
    k = qkv[..., D_MODEL:2 * D_MODEL].reshape(B, S, A_HEADS, 2, A_DHEAD)
    v = qkv[..., 2 * D_MODEL:].reshape(B, S, A_HEADS, A_VDIM)
    q = rms_norm(q, g_q) * (A_DHEAD ** -0.5)
    k = rms_norm(k, g_k)
    lam = (jnp.exp(jnp.sum(lam_q1.astype(jnp.float32) * lam_k1.astype(jnp.float32)))
           - jnp.exp(jnp.sum(lam_q2.astype(jnp.float32) * lam_k2.astype(jnp.float32)))
           + lambda_init)
    slopes = alibi_slopes(A_HEADS)
    nb = S // Q_BLOCK
    q_blocks = q.reshape(B, nb, Q_BLOCK, A_HEADS, 2, A_DHEAD).transpose(1, 0, 2, 3, 4, 5)
    key_pos = jnp.arange(S)

    def one_block(args):
        qb, bi = args
        q_pos = bi * Q_BLOCK + jnp.arange(Q_BLOCK)
        s = jnp.einsum('bqhcd,bkhcd->bhcqk', qb, k).astype(jnp.float32)
        dist = q_pos[:, None] - key_pos[None, :]
        bias = -slopes[:, None, None] * dist.astype(jnp.float32)
        s = s + bias[None, :, None]
        s = jnp.where(dist[None, None, None] >= 0, s, -jnp.inf)
        a = jax.nn.softmax(s, axis=-1)
        attn = a[:, :, 0] - lam * a[:, :, 1]
        return jnp.einsum('bhqk,bkhe->bqhe', attn.astype(v.dtype), v)

    o = lax.map(one_block, (q_blocks, jnp.arange(nb)))
    o = o.transpose(1, 0, 2, 3, 4).reshape(B, S, A_HEADS, A_VDIM)
    o = rms_norm(o, g_sub) * (1.0 - lambda_init)
    return o.reshape(B, S, A_HEADS * A_VDIM) @ w_o


def retention(xn, w_in, w_out, g_gn):
    B, S, _ = xn.shape
    dt = xn.dtype
    proj = xn @ w_in
    qw, kw, vw = R_HEADS * R_DK, R_HEADS * R_DK, R_HEADS * R_DV
    q = proj[..., :qw]
    k = proj[..., qw:qw + kw] * (R_DK ** -0.5)
    v = proj[..., qw + kw:qw + kw + vw]
    gate = proj[..., qw + kw + vw:]
    N = S // CHUNK
    qc = q.reshape(B, N, CHUNK, R_HEADS, R_DK)
    kc = k.reshape(B, N, CHUNK, R_HEADS, R_DK)
    vc = v.reshape(B, N, CHUNK, R_HEADS, R_DV)

    gamma = 1.0 - 2.0 ** (-5.0 - jnp.arange(R_HEADS, dtype=jnp.float32))
    log_g = jnp.log(gamma)
    pos = jnp.arange(CHUNK, dtype=jnp.float32)
    diff = pos[:, None] - pos[None, :]
    intra_decay = jnp.where(diff[None] >= 0,
                            jnp.exp(jnp.maximum(diff, 0.0)[None] * log_g[:, None, None]),
                            0.0).astype(dt)
    q_dec = jnp.exp((pos[:, None] + 1.0) * log_g[None, :]).astype(dt)
    k_dec = jnp.exp((CHUNK - 1.0 - pos[:, None]) * log_g[None, :]).astype(dt)
    chunk_dec = jnp.exp(CHUNK * log_g).astype(dt)

    sc = jnp.einsum('bnihd,bnjhd->bnhij', qc, kc) * intra_decay[None, None]
    intra = jnp.einsum('bnhij,bnjhe->bnihe', sc, vc)

    def step(R, xs):
        q_n, k_n, v_n = xs
        cross = jnp.einsum('bihd,bhde->bihe', q_n, R) * q_dec[None, :, :, None]
        R_new = (chunk_dec[None, :, None, None] * R
                 + jnp.einsum('bjhd,bjhe->bhde', k_n * k_dec[None, :, :, None], v_n))
        return R_new, cross

    R0 = jnp.zeros((B, R_HEADS, R_DK, R_DV), dtype=dt)
    _, cross = lax.scan(step, R0, (qc.transpose(1, 0, 2, 3, 4),
                                   kc.transpose(1, 0, 2, 3, 4),
                                   vc.transpose(1, 0, 2, 3, 4)))
    o = intra + cross.transpose(1, 0, 2, 3, 4)
    o = rms_norm(o.reshape(B, S, R_HEADS, R_DV), g_gn).reshape(B, S, R_HEADS * R_DV)
    return (jax.nn.silu(gate) * o) @ w_out


def sq_relu_mlp(xn, w1, w2):
    h = jax.nn.relu(xn @ w1)
    return (h * h) @ w2


def setup_inputs(seed: int = 0) -> dict:
    key = jax.random.key(seed)
    ks = jax.random.split(key, 24)
    f32 = jnp.float32

    def nrm(k, shape, scale):
        return jax.random.normal(k, shape, f32) * scale

    def gain(k, shape):
        return 1.0 + 0.02 * jax.random.normal(k, shape, f32)

    return {
        "x": nrm(ks[0], (BATCH, SEQ, D_MODEL), 1.0),
        "p": nrm(ks[1], (DEPTH, BATCH, SEQ, PLE_DIM), 1.0),
        "norm_mix": gain(ks[2], (DEPTH, D_MODEL)),
        "norm_mlp": gain(ks[3], (DEPTH, D_MODEL)),
        "norm_pe": gain(ks[4], (DEPTH, D_MODEL)),
        "a_w_qkv": nrm(ks[5], (N_A_LAYERS, D_MODEL, 3 * D_MODEL), D_MODEL ** -0.5),
        "a_w_o": nrm(ks[6], (N_A_LAYERS, A_HEADS * A_VDIM, D_MODEL), (A_HEADS * A_VDIM) ** -0.5),
        "a_g_q": gain(ks[7], (N_A_LAYERS, A_DHEAD)),
        "a_g_k": gain(ks[8], (N_A_LAYERS, A_DHEAD)),
        "a_lam_q1": nrm(ks[9], (N_A_LAYERS, A_DHEAD), 0.1),
        "a_lam_k1": nrm(ks[10], (N_A_LAYERS, A_DHEAD), 0.1),
        "a_lam_q2": nrm(ks[11], (N_A_LAYERS, A_DHEAD), 0.1),
        "a_lam_k2": nrm(ks[12], (N_A_LAYERS, A_DHEAD), 0.1),
        "a_g_sub": gain(ks[13], (N_A_LAYERS, A_VDIM)),
        "r_w_in": nrm(ks[14], (N_B_LAYERS, D_MODEL, 2 * R_HEADS * R_DK + 2 * R_HEADS * R_DV), D_MODEL ** -0.5),
        "r_w_out": nrm(ks[15], (N_B_LAYERS, R_HEADS * R_DV, D_MODEL), (R_HEADS * R_DV) ** -0.5),
        "r_g_gn": gain(ks[16], (N_B_LAYERS, R_DV)),
        "mlp_w1": nrm(ks[17], (DEPTH, D_MODEL, D_FF), D_MODEL ** -0.5),
        "mlp_w2": nrm(ks[18], (DEPTH, D_FF, D_MODEL), 0.5 * D_FF ** -0.5),
        "pe_w_up": nrm(ks[19], (DEPTH, PLE_DIM, D_MODEL), PLE_DIM ** -0.5),
        "pe_w_gate": nrm(ks[20], (DEPTH, D_MODEL, D_MODEL), D_MODEL ** -0.5),
    }


def reference(x, p, norm_mix, norm_mlp, norm_pe,
              a_w_qkv, a_w_o, a_g_q, a_g_k, a_lam_q1, a_lam_k1, a_lam_q2, a_lam_k2, a_g_sub,
              r_w_in, r_w_out, r_g_gn,
              mlp_w1, mlp_w2, pe_w_up, pe_w_gate):
    for i in range(DEPTH):
        j = i // N_MIXERS
        h = rms_norm(x, norm_mix[i])
        if i % N_MIXERS == 0:
            lambda_init = 0.8 - 0.6 * math.exp(-0.3 * i)
            mix = diff_attention(h, a_w_qkv[j], a_w_o[j], a_g_q[j], a_g_k[j],
                                 a_lam_q1[j], a_lam_k1[j], a_lam_q2[j], a_lam_k2[j],
                                 a_g_sub[j], lambda_init)
        else:
            mix = retention(h, r_w_in[j], r_w_out[j], r_g_gn[j])
        x = x + mix
        x = x + sq_relu_mlp(rms_norm(x, norm_mlp[i]), mlp_w1[i], mlp_w2[i])
        gate = jax.nn.sigmoid(rms_norm(x, norm_pe[i]) @ pe_w_gate[i])
        x = x + gate * (p[i] @ pe_w_up[i])
    return x
```

```python
import math
from contextlib import ExitStack

import numpy as np
import ml_dtypes

import concourse.bass as bass
import concourse.mybir as mybir
from concourse.bass_utils import run_bass_kernel_spmd

F32 = mybir.dt.float32
BF16 = mybir.dt.bfloat16
AF = mybir.ActivationFunctionType
ALU = mybir.AluOpType

N_CORES = 8
SEQ_PER_CORE = 2
S = 2048
D = 1024
NT = S // 128
KC = D // 128
DEPTH = 4
DFF = 4096
PLE = 256
EPS = 1e-6
A_HEADS = 8
R_HEADS = 4
NSLAB = 5
SLAB_COLS = 4096
CSHIFT = 4.0


class Sem:
    def __init__(self, nc, es, name):
        self.h = es.enter_context(nc.semaphore(name))
        self.n = 0
        self.name = name
        self.sig = 0
        self.rank = {}


class Res:
    __slots__ = ("name", "w", "r")

    def __init__(self, name):
        self.name = name
        self.w = None
        self.r = {}


class EngQ:
    def __init__(self, nc, es, eng, name, is_pe=False, compute=True):
        self.e = eng
        self.name = name
        self.is_pe = is_pe
        self.sem = Sem(nc, es, "q_" + name) if compute else None
        if self.sem is not None:
            self.sem.is_dma = False
        self.waited = {}
        self.log = None

    def wait(self, toks):
        best = {}
        for sem, val in toks:
            if best.get(sem, 0) < val:
                best[sem] = val
        for sem, val in best.items():
            if self.waited.get(sem, 0) >= val:
                continue
            if self.is_pe and sem is self.sem:
                continue
            if sem.is_dma:
                self.e.wait_ge(sem.h, val)
            else:
                self.log.setdefault(sem.name, set()).add(val)
                self.e.wait_ge(sem.h, sem.rank.get(val, val))
            self.waited[sem] = val


class Ctx:
    def __init__(self, nc, es, sig=None):
        self.nc = nc
        self.sig = sig
        self.log = {}
        self.q = {
            "pe": EngQ(nc, es, nc.tensor, "pe", is_pe=True),
            "act": EngQ(nc, es, nc.scalar, "act"),
            "dve": EngQ(nc, es, nc.vector, "dve"),
            "pool": EngQ(nc, es, nc.gpsimd, "pool"),
            "sp": EngQ(nc, es, nc.sync, "sp", compute=False),
        }
        for E in self.q.values():
            E.log = self.log

    @staticmethod
    def _deps(reads, writes):
        toks = []
        for r in reads:
            if r.w is not None:
                toks.append(r.w)
        for w in writes:
            if w.w is not None:
                toks.append(w.w)
            toks.extend(w.r.items())
        return toks

    def op(self, eng, fn, reads=(), writes=()):
        E = self.q[eng]
        E.wait(self._deps(reads, writes))
        inst = fn(E.e)
        E.sem.n += 1
        if self.sig is None or E.sem.n in self.sig.get(E.sem.name, ()):
            inst.then_inc(E.sem.h, 1)
            E.sem.sig += 1
            if self.sig is not None:
                E.sem.rank[E.sem.n] = E.sem.sig
        tok = (E.sem, E.sem.n)
        for r in reads:
            r.r[E.sem] = E.sem.n
        for w in writes:
            w.w = tok
            w.r = {}
        return tok

    def dma(self, eng, pairs, sem, reads=(), writes=()):
        E = self.q[eng]
        E.wait(self._deps(reads, writes))
        for out, in_ in pairs:
            E.e.dma_start(out=out, in_=in_).then_inc(sem.h, 16)
            sem.n += 16
        tok = (sem, sem.n)
        for r in reads:
            r.r[sem] = sem.n
        for w in writes:
            w.w = tok
            w.r = {}
        return tok

    def barrier(self):
        comp = [self.q[k] for k in ("pe", "act", "dve", "pool")]
        for E in self.q.values():
            E.wait([(F.sem, F.sem.n) for F in comp if F is not E and F.sem.n > 0])


def _const_tables():
    bf = ml_dtypes.bfloat16
    idx = np.arange(128)
    ident = np.eye(128, dtype=np.float32)
    tri = (idx[None, :] >= idx[:, None]).astype(np.float32)
    bd = np.zeros((128, 128), np.float32)
    bd[:64, :64] = 1.0
    bd[64:, 64:] = 1.0
    cb = np.concatenate([ident, tri, bd], axis=1).astype(bf)

    gam = 1.0 - 2.0 ** (-5.0 - np.arange(R_HEADS, dtype=np.float64))
    j = idx.astype(np.float64)
    dt = np.zeros((128, R_HEADS, 128), np.float64)
    for h in range(R_HEADS):
        dt[:, h, :] = (gam[h] ** (-(j[:, None] + 1.0))) / 16.0 * (idx[None, :] >= idx[:, None])
    qdec = gam[None, :] ** (j[:, None] + 1.0)
    kdec16 = gam[None, :] ** (127.0 - j[:, None]) / 16.0
    cf = np.concatenate([dt.reshape(128, -1), qdec, kdec16], axis=1).astype(np.float32)
    chunk_dec = [float(g ** 128.0) for g in gam]

    pos = np.arange(S)
    blk = (pos // 128).astype(np.float64)
    inn = (pos % 128).astype(np.float64)
    qaug = np.zeros((A_HEADS, 4, S), np.float64)
    kaug = np.zeros((A_HEADS, 4, S), np.float64)
    for h in range(A_HEADS):
        sl = 2.0 ** (-8.0 * (h + 1) / A_HEADS)
        qaug[h, 0] = -sl * 128.0 * blk
        qaug[h, 1] = -sl * inn
        qaug[h, 2] = 1.0
        qaug[h, 3] = 1.0
        kaug[h, 0] = 1.0
        kaug[h, 1] = 1.0
        kaug[h, 2] = sl * 128.0 * blk
        kaug[h, 3] = sl * inn
    aug = np.stack([qaug, kaug], 0).astype(np.float32)
    augb = aug.astype(bf)
    assert np.array_equal(augb.astype(np.float32), aug)
    return cb, cf, augb, chunk_dec


CF_DT, CF_QDEC, CF_KDEC = 0, 512, 516
NCF = 520
PF_G = 0
PF_A = 96
PF_A_STRIDE = 259
PF_B = PF_A + 2 * PF_A_STRIDE
NPF = PF_B + 2 * 4


def _pack_params(inp):
    pf = np.zeros((128, NPF), np.float32)
    norms = (inp["norm_mix"], inp["norm_mlp"], inp["norm_pe"])
    for L in range(DEPTH):
        for w in range(3):
            c = PF_G + (L * 3 + w) * 8
            pf[:, c:c + 8] = np.asarray(norms[w][L], np.float32).reshape(8, 128).T
    for j in range(2):
        c = PF_A + j * PF_A_STRIDE
        pf[:, c] = np.tile(np.asarray(inp["a_g_q"][j], np.float32), 2)
        pf[:, c + 1] = np.tile(np.asarray(inp["a_g_k"][j], np.float32), 2)
        pf[:, c + 2] = np.asarray(inp["a_g_sub"][j], np.float32)
        lam = np.concatenate([np.asarray(inp[k][j], np.float32) for k in
                              ("a_lam_q1", "a_lam_k1", "a_lam_q2", "a_lam_k2")])
        pf[:, c + 3:c + 259] = lam[None, :]
    for j in range(2):
        c = PF_B + j * 4
        pf[:, c:c + 4] = np.asarray(inp["r_g_gn"][j], np.float32).reshape(4, 128).T
    return pf


class Builder:
    def __init__(self, layers, nseq=SEQ_PER_CORE, sig=None):
        self.layers = list(layers)
        self.nseq = nseq
        self.sig = sig
        self.nc = bass.Bass("TRN2", target_bir_lowering=False)
        _, _, _, self.chunk_dec = _const_tables()

    def _dram(self):
        nc = self.nc
        di = lambda n, shp, dt=F32: nc.dram_tensor(n, list(shp), dt, kind="ExternalInput").ap()
        self.d_x = di("x", [self.nseq, S, D])
        self.d_p = di("p", [DEPTH, self.nseq, S, PLE])
        self.d_wqkv = di("a_w_qkv", [2, D, 3 * D])
        self.d_wo = di("a_w_o", [2, D, D])
        self.d_win = di("r_w_in", [2, D, 6 * D])
        self.d_wout = di("r_w_out", [2, 2 * D, D])
        self.d_w1 = di("mlp_w1", [DEPTH, D, DFF])
        self.d_w2 = di("mlp_w2", [DEPTH, DFF, D])
        self.d_wup = di("pe_w_up", [DEPTH, PLE, D])
        self.d_wgate = di("pe_w_gate", [DEPTH, D, D])
        self.d_pf = di("params_f32", [128, NPF])
        self.d_cf = di("consts_f32", [128, NCF])
        self.d_cb = di("consts_bf16", [128, 384], BF16)
        self.d_aug = di("aug_bf16", [2, A_HEADS, 4, S], BF16)
        self.d_y = nc.dram_tensor("y", [self.nseq, S, D], F32, kind="ExternalOutput").ap()

    def _alloc(self, es):
        nc = self.nc
        sb = lambda n, shp, dt: es.enter_context(nc.sbuf_tensor(n, list(shp), dt))
        self.X = sb("X", [128, NT, D], F32)
        self.XNT = sb("XNT", [128, KC, S], BF16)
        self.ARENA = sb("ARENA", [128, 16384], BF16)
        self.SLAB = [sb(f"SLAB{i}", [128, SLAB_COLS], BF16) for i in range(NSLAB)]
        self.PFt = sb("PF", [128, NPF], F32)
        self.CFt = sb("CF", [128, NCF], F32)
        self.CBt = sb("CB", [128, 384], BF16)
        self.PREP = sb("PREP", [128, 16], F32)
        self.LAMT = sb("LAMT", [128, 128], F32)
        self.JUNK = sb("JUNK", [128, D], BF16)
        self.XNP = [sb(f"XNP{i}", [128, D], BF16) for i in range(2)]
        self.STAT = sb("STAT", [128, 64], F32)
        self.SCRF = [sb(f"SCRF{i}", [128, 512], F32) for i in range(4)]
        self.SCRB = [sb(f"SCRB{i}", [128, 512], BF16) for i in range(6)]
        self.R32 = sb("R32", [128, 2, 512], F32)
        self.RB = sb("RB", [128, 2, 512], BF16)
        ps = lambda n, shp, dt: es.enter_context(nc.psum_tensor(n, list(shp), dt))
        self.PS = [ps(f"PS{i}", [128, 512], F32) for i in range(8)]
        self.SCRG = [sb(f"SCRG{i}", [128, 512], F32) for i in range(2)]
        self.r_x = [Res(f"x{t}") for t in range(NT)]
        self.r_xnt = [Res(f"xnt{g}") for g in range(4)]
        self.r_slab = [Res(f"slab{i}") for i in range(NSLAB)]
        self.r_ps = [Res(f"ps{i}") for i in range(8)]
        self.r_scrg = [Res("scrg0"), Res("scrg1")]
        self.r_junk = Res("junk")
        self.r_xnp = [Res("xnp0"), Res("xnp1")]
        self.r_stat = [Res(f"stat{i}") for i in range(64)]
        self.r_scrf = [Res(f"scrf{i}") for i in range(4)]
        self.r_scrb = [Res(f"scrb{i}") for i in range(6)]
        self.r_const = Res("const")
        self.r_prep = Res("prep")
        self.r_r32 = Res("r32")
        self.r_rb = Res("rb")
        self.ps_rr = 0
        self.s_slab = [Sem(nc, es, f"d_slab{i}") for i in range(NSLAB)]
        self.s_x = Sem(nc, es, "d_x")
        self.s_y = Sem(nc, es, "d_y")
        self.s_c = Sem(nc, es, "d_c")
        self.s_p = Sem(nc, es, "d_p")
        self.s_aug = Sem(nc, es, "d_aug")
        self.s_xl = [Sem(nc, es, f"d_xl{t}") for t in range(NT)]
        self.s_xs = [Sem(nc, es, f"d_xs{t}") for t in range(NT)]
        for sm in self.s_slab + [self.s_x, self.s_y, self.s_c, self.s_p, self.s_aug] + self.s_xl + self.s_xs:
            sm.is_dma = True
        self.ident = self.CBt[:, 0:128]
        self.tri = self.CBt[:, 128:256]
        self.bd = self.CBt[:, 256:384]

    def next_ps(self, n=6):
        i = self.ps_rr % n
        self.ps_rr = (i + 1) % n
        return i

    def psb(self, i):
        return self.PS[i][:].bitcast(BF16)

    def _weight_plan(self):
        plan = []
        for s in range(self.nseq):
            for L in self.layers:
                j = L // 2
                if L % 2 == 0:
                    w = self.d_wqkv[j].rearrange("(kc p) c -> p kc c", p=128)
                    for h in range(A_HEADS):
                        parts = []
                        for sec in range(3):
                            parts.append(((sec * 128, 128), w[:, :, sec * D + h * 128: sec * D + (h + 1) * 128]))
                        plan.append((("qkv", s, L, h), 384, parts))
                        plan.append((("wo", s, L, h), 1024, [((0, 1024), self.d_wo[j][h * 128:(h + 1) * 128, :])]))
                else:
                    w = self.d_win[j].rearrange("(kc p) c -> p kc c", p=128)
                    wo = self.d_wout[j].rearrange("(ec p) c -> p ec c", p=128)
                    for h in range(R_HEADS):
                        plan.append((("rqk", s, L, h), 512, [((0, 256), w[:, :, h * 256:(h + 1) * 256]),
                                                              ((256, 256), w[:, :, D + h * 256: D + (h + 1) * 256])]))
                        plan.append((("rv", s, L, h), 512, [((0, 512), w[:, :, 2 * D + h * 512: 2 * D + (h + 1) * 512])]))
                        plan.append((("rg", s, L, h), 512, [((0, 512), w[:, :, 4 * D + h * 512: 4 * D + (h + 1) * 512])]))
                        plan.append((("rwo", s, L, h), 1024, [((0, 1024), wo[:, h * 4:(h + 1) * 4, :])]))
                w1 = self.d_w1[L].rearrange("(kc p) f -> p kc f", p=128)
                w2 = self.d_w2[L].rearrange("(fc p) d -> p fc d", p=128)
                for sl in range(8):
                    plan.append((("w1", s, L, sl), 512, [((0, 512), w1[:, :, sl * 512:(sl + 1) * 512])]))
                    plan.append((("w2", s, L, sl), 1024, [((0, 1024), w2[:, sl * 4:(sl + 1) * 4, :])]))
                wg = self.d_wgate[L].rearrange("(kc p) c -> p kc c", p=128)
                wu = self.d_wup[L].rearrange("(kc p) c -> p kc c", p=128)
                for half in range(2):
                    plan.append((("wg", s, L, half), 512, [((0, 512), wg[:, :, half * 512:(half + 1) * 512])]))
                plan.append((("wu", s, L, 0), 1024, [((0, 1024), wu)]))
        return plan

    def w_init(self):
        self.plan = self._weight_plan()
        self.w_next_issue = 0
        self.w_next_get = 0
        self.w_free = list(range(NSLAB))
        self.w_loc = {}

    def w_pump(self):
        ctx = self.ctx
        while self.w_free and self.w_next_issue < len(self.plan):
            key, rowlen, parts = self.plan[self.w_next_issue]
            b = self.w_free.pop(0)
            pairs = []
            for (c0, cn), dap in parts:
                n_outer = dap.shape[1] if len(dap.shape) == 3 else 1
                if len(dap.shape) == 3:
                    view = self.SLAB[b][:, 0:n_outer * rowlen].rearrange("p (k c) -> p k c", c=rowlen)[:, :, c0:c0 + cn]
                else:
                    view = self.SLAB[b][:, c0:c0 + cn]
                pairs.append((view, dap))
            ctx.dma("pool", pairs, self.s_slab[b], writes=[self.r_slab[b]])
            self.w_loc[self.w_next_issue] = b
            self.w_next_issue += 1

    def w_get(self, key):
        self.w_pump()
        i = self.w_next_get
        assert self.plan[i][0] == key, (self.plan[i][0], key)
        assert i in self.w_loc, "weight slab pool deadlock: " + str(key)
        self.w_next_get += 1
        b = self.w_loc.pop(i)
        return b

    def w_rel(self, b):
        self.w_free.append(b)
        self.w_pump()

    def slab3(self, b, rowlen, k):
        return self.SLAB[b][:, 0:k * rowlen].rearrange("p (k c) -> p k c", c=rowlen)

    def rmsnorm_to_xnt(self, L, which, cb=None, stats_first=False):
        ctx = self.ctx
        gcol = self.PFt[:, PF_G + (L * 3 + which) * 8: PF_G + (L * 3 + which) * 8 + 8]
        gb = gcol.unsqueeze(2).broadcast_to([128, 8, 128])

        def stats(t):
            ss = self.STAT[:, t:t + 1]
            rs = self.STAT[:, 16 + t:17 + t]
            r_ss, r_rs = self.r_stat[t], self.r_stat[16 + t]
            ctx.op("act", lambda e: e.activation(out=self.JUNK[:], in_=self.X[:, t, :], func=AF.Square,
                                                 accum_out=ss),
                   reads=[self.r_x[t]], writes=[self.r_junk, r_ss])
            ctx.op("act", lambda e: e.activation(out=rs, in_=ss, func=AF.Sqrt, scale=1.0 / D, bias=EPS),
                   reads=[r_ss], writes=[r_rs])

        def scale(t):
            b = t % 2
            rs = self.STAT[:, 16 + t:17 + t]
            r_rs = self.r_stat[16 + t]
            ctx.op("dve", lambda e: e.reciprocal(out=rs, in_=rs), writes=[r_rs])
            ctx.op("dve", lambda e: e.tensor_scalar(out=self.XNP[b][:], in0=self.X[:, t, :], scalar1=rs,
                                                    scalar2=None, op0=ALU.mult),
                   reads=[self.r_x[t], r_rs], writes=[self.r_xnp[b]])

        def transp(t):
            b = t % 2

            def tr(e):
                last = None
                for kc in range(KC):
                    last = e.transpose(self.psb(6 + b)[:, kc * 128:(kc + 1) * 128],
                                       self.XNP[b][:, kc * 128:(kc + 1) * 128], self.ident)
                return last
            ctx.op("pe", tr, reads=[self.r_xnp[b], self.r_const], writes=[self.r_ps[6 + b]])

        def evac(t):
            b = t % 2
            ctx.op("dve", lambda e: e.tensor_tensor(
                out=self.XNT[:, :, t * 128:(t + 1) * 128],
                in0=self.psb(6 + b).rearrange("p (k c) -> p k c", k=8), in1=gb, op=ALU.mult),
                reads=[self.r_ps[6 + b], self.r_const], writes=[self.r_xnt[t // 4]])

        if stats_first:
            for t in range(NT):
                stats(t)
        else:
            stats(0)
            stats(1)
        scale(0)
        for t in range(NT):
            if t + 2 < NT and not stats_first:
                stats(t + 2)
            transp(t)
            if t + 1 < NT:
                scale(t + 1)
            evac(t)
            if cb is not None:
                cb(t)

    def mlp(self, s, L):
        ctx = self.ctx
        HT = [self.ARENA[:, i * 8192:(i + 1) * 8192].rearrange("p (f t) -> p f t", f=4) for i in range(2)]
        r_ht = [Res("ht0"), Res("ht1")]
        slabs = {}
        w1s = {}

        def up_begin(sl):
            w1s[sl] = self.w_get(("w1", s, L, sl))
            slabs[sl] = self.w_get(("w2", s, L, sl))

        def up_part(sl, order):
            hb = sl % 2
            b1 = w1s[sl]
            w1 = self.slab3(b1, 512, 8)
            for fc, g in order:
                if True:
                    pi = self.next_ps()

                    def mm(e):
                        last = None
                        for kc in range(KC):
                            last = e.matmul(self.PS[pi][:], lhsT=w1[:, kc, fc * 128:(fc + 1) * 128],
                                            rhs=self.XNT[:, kc, g * 512:(g + 1) * 512],
                                            start=(kc == 0), stop=(kc == KC - 1))
                        return last
                    ctx.op("pe", mm, reads=[self.r_slab[b1], self.r_xnt[g]], writes=[self.r_ps[pi]])
                    sf = (fc * 4 + g) % 4
                    ctx.op("act", lambda e: e.activation(out=self.SCRF[sf][:], in_=self.PS[pi][:], func=AF.Relu),
                           reads=[self.r_ps[pi]], writes=[self.r_scrf[sf]])
                    eng = "dve" if (fc * 4 + g) % 2 == 0 else "pool"
                    ctx.op(eng, lambda e: e.tensor_tensor(out=HT[hb][:, fc, g * 512:(g + 1) * 512],
                                                          in0=self.SCRF[sf][:], in1=self.SCRF[sf][:], op=ALU.mult),
                           reads=[self.r_scrf[sf]], writes=[r_ht[hb]])

        def up(sl):
            up_begin(sl)
            up_part(sl, [(fc, g) for fc in range(4) for g in range(4)])
            self.w_rel(w1s.pop(sl))

        def down(sl):
            hb = sl % 2
            b2 = slabs.pop(sl)
            w2 = self.slab3(b2, 1024, 4)
            for t in range(NT):
                for half in range(2):
                    pi = self.next_ps()

                    def mm(e):
                        last = None
                        for fc in range(4):
                            last = e.matmul(self.PS[pi][:], lhsT=HT[hb][:, fc, t * 128:(t + 1) * 128],
                                            rhs=w2[:, fc, half * 512:(half + 1) * 512],
                                            start=(fc == 0), stop=(fc == 3))
                        return last
                    ctx.op("pe", mm, reads=[self.r_slab[b2], r_ht[hb]], writes=[self.r_ps[pi]])
                    xs = self.X[:, t, half * 512:(half + 1) * 512]
                    ctx.op("dve", lambda e: e.tensor_tensor(out=xs, in0=self.PS[pi][:], in1=xs, op=ALU.add),
                           reads=[self.r_ps[pi]], writes=[self.r_x[t]])
            self.w_rel(b2)

        up_begin(0)
        self.rmsnorm_to_xnt(L, 1, cb=lambda t: up_part(0, [(t % 4, t // 4 - 1)]) if t >= 4 else None)
        up_part(0, [(fc, 3) for fc in range(4)])
        self.w_rel(w1s.pop(0))
        for sl in range(8):
            if sl + 1 < 8:
                up(sl + 1)
            down(sl)

    def ple(self, s, L, final=False):
        ctx = self.ctx
        PBF = self.ARENA[:, 0:4096].rearrange("p (t c) -> p t c", t=NT)
        PT = self.ARENA[:, 4096:8192].rearrange("p (k t) -> p k t", k=2)
        r_pbf, r_pt = Res("pbf"), Res("pt")
        ctx.dma("pool", [(PBF, self.d_p[L, s].rearrange("(t p) c -> p t c", p=128))], self.s_p, writes=[r_pbf])
        for tg in range(4):
            b = tg % 2

            def tr(e):
                last = None
                for tt in range(4):
                    for c2 in range(2):
                        last = e.transpose(self.psb(6 + b)[:, (tt * 2 + c2) * 128:(tt * 2 + c2 + 1) * 128],
                                           PBF[:, tg * 4 + tt, c2 * 128:(c2 + 1) * 128], self.ident)
                return last
            ctx.op("pe", tr, reads=[r_pbf, self.r_const], writes=[self.r_ps[6 + b]])
            for c2 in range(2):
                ctx.op("dve", lambda e: e.tensor_copy(
                    out=PT[:, c2, tg * 512:(tg + 1) * 512].rearrange("p (t c) -> p t c", t=4),
                    in_=self.psb(6 + b).rearrange("p (t k c) -> p t k c", t=4, k=2)[:, :, c2, :]),
                    reads=[self.r_ps[6 + b]], writes=[r_pt])
        bg = [self.w_get(("wg", s, L, half)) for half in range(2)]
        bu = self.w_get(("wu", s, L, 0))
        wu = self.slab3(bu, 1024, 2)
        def tile_work(t):
            for half in range(2):
                wg = self.slab3(bg[half], 512, 8)
                pg = self.next_ps()

                def mmg(e):
                    last = None
                    for kc in range(KC):
                        last = e.matmul(self.PS[pg][:], lhsT=self.XNT[:, kc, t * 128:(t + 1) * 128],
                                        rhs=wg[:, kc, :], start=(kc == 0), stop=(kc == KC - 1))
                    return last
                ctx.op("pe", mmg, reads=[self.r_slab[bg[half]], self.r_xnt[t // 4]], writes=[self.r_ps[pg]])
                pu = self.next_ps()

                def mmu(e):
                    last = None
                    for c2 in range(2):
                        last = e.matmul(self.PS[pu][:], lhsT=PT[:, c2, t * 128:(t + 1) * 128],
                                        rhs=wu[:, c2, half * 512:(half + 1) * 512], start=(c2 == 0), stop=(c2 == 1))
                    return last
                ctx.op("pe", mmu, reads=[self.r_slab[bu], r_pt], writes=[self.r_ps[pu]])
                sf = (t * 2 + half) % 4
                ctx.op("act", lambda e: e.activation(out=self.SCRF[sf][:], in_=self.PS[pg][:], func=AF.Sigmoid),
                       reads=[self.r_ps[pg]], writes=[self.r_scrf[sf]])
                ctx.op("dve", lambda e: e.tensor_tensor(out=self.SCRF[sf][:], in0=self.PS[pu][:],
                                                        in1=self.SCRF[sf][:], op=ALU.mult),
                       reads=[self.r_ps[pu]], writes=[self.r_scrf[sf]])
                xs = self.X[:, t, half * 512:(half + 1) * 512]
                ctx.op("pool", lambda e: e.tensor_tensor(out=xs, in0=self.SCRF[sf][:], in1=xs, op=ALU.add),
                       reads=[self.r_scrf[sf]], writes=[self.r_x[t]])
            if final:
                self.x_store(s, t)
                if s + 1 < self.nseq and t >= 1:
                    self.x_load(s + 1, t - 1)
        self.rmsnorm_to_xnt(L, 2, cb=lambda t: tile_work(t - 4) if t >= 4 else None, stats_first=True)
        for t in range(NT - 4, NT):
            tile_work(t)
        if final and s + 1 < self.nseq:
            self.x_load(s + 1, NT - 1)
        for b in bg:
            self.w_rel(b)
        self.w_rel(bu)

    def attention(self, s, L):
        ctx = self.ctx
        j = L // 2
        A = self.ARENA
        QA = [A[:, 0:2048], A[:, 2048:4096]]
        KA = [A[:, 4096:6144], A[:, 6144:8192]]
        VA = A[:, 8192:8192 + NT * 129].rearrange("p (t c) -> p t c", c=129)
        OT = [A[:, 12288:14336], A[:, 14336:16384]]
        r_qa = [Res("qa0"), Res("qa1")]
        r_ka = [Res("ka0"), Res("ka1")]
        r_va = Res("va")
        r_ot = [Res("ot0"), Res("ot1")]
        ctx.op("dve", lambda e: e.memset(VA[:, :, 128:129], 1.0), writes=[r_va])
        for c in range(2):
            ctx.op("dve", lambda e: e.memset(QA[c][64:128, :], 0.0), writes=[r_qa[c]])
            ctx.op("dve", lambda e: e.memset(KA[c][64:128, :], 0.0), writes=[r_ka[c]])
        pa = PF_A + j * PF_A_STRIDE
        gq = self.PREP[:, 2 * j:2 * j + 1]
        gk = self.PFt[:, pa + 1:pa + 2]
        gsub = self.PREP[:, 4 + j:5 + j]
        neglam = self.PREP[:, 8 + j:9 + j]
        OB = [0, 1, 2, 3]
        SBK = [4, 5]
        PJ = [4, 5, 0, 1]
        SUMS = [2, 3]
        WOB = 6
        TRB = 7
        filler = []

        def fill(n):
            for _ in range(n):
                if filler:
                    filler.pop(0)[1]()

        def drain(upto_head):
            while filler and filler[0][0] <= upto_head:
                filler.pop(0)[1]()

        pending = []

        def step():
            if pending:
                pending.pop(0)()

        def flush_post():
            while pending:
                pending.pop(0)()

        for h in range(A_HEADS):
            hb = h % 2
            bw = self.w_get(("qkv", s, L, h))
            bo = self.w_get(("wo", s, L, h))
            w = self.slab3(bw, 384, 8)
            ctx.dma("sp", [(QA[0][64:68, :], self.d_aug[0, h]), (QA[1][64:68, :], self.d_aug[0, h]),
                           (KA[0][64:68, :], self.d_aug[1, h]), (KA[1][64:68, :], self.d_aug[1, h])],
                    self.s_aug, writes=r_qa + r_ka)
            tiles = [(qk, g) for g in range(4) for qk in range(2)] if h == 0 else \
                    [(qk, g) for qk in range(2) for g in range(4)]

            def proj_mm(i):
                qk, g = tiles[i]
                pi = PJ[i % 4]

                def mm(e):
                    last = None
                    for kc in range(KC):
                        last = e.matmul(self.PS[pi][:], lhsT=w[:, kc, qk * 128:(qk + 1) * 128],
                                        rhs=self.XNT[:, kc, g * 512:(g + 1) * 512],
                                        start=(kc == 0), stop=(kc == KC - 1))
                    return last
                ctx.op("pe", mm, reads=[self.r_slab[bw], self.r_xnt[g]], writes=[self.r_ps[pi]])
                ctx.op("act", lambda e: e.activation(out=self.SCRB[i % 2][:], in_=self.PS[pi][:], func=AF.Square),
                       reads=[self.r_ps[pi]], writes=[self.r_scrb[i % 2]])

            def norm_rest(i):
                qk, g = tiles[i]
                pi = PJ[i % 4]
                p2 = SUMS[i % 2]
                dst = QA if qk == 0 else KA
                rdst = r_qa if qk == 0 else r_ka
                gcol = gq if qk == 0 else gk
                ctx.op("pe", lambda e: e.matmul(self.PS[p2][:], lhsT=self.bd, rhs=self.SCRB[i % 2][:],
                                                start=True, stop=True),
                       reads=[self.r_scrb[i % 2], self.r_const], writes=[self.r_ps[p2]])
                G = self.SCRG[i % 2]
                ctx.op("act", lambda e: e.activation(out=G[:], in_=self.PS[p2][:], func=AF.Ln,
                                                     scale=1.0 / 64, bias=EPS),
                       reads=[self.r_ps[p2]], writes=[self.r_scrg[i % 2]])
                ctx.op("act", lambda e: e.activation(out=G[:], in_=G[:], func=AF.Exp, scale=-0.5),
                       writes=[self.r_scrg[i % 2]])
                for c in range(2):
                    ctx.op("dve", lambda e: e.scalar_tensor_tensor(
                        out=dst[c][0:64, g * 512:(g + 1) * 512], in0=self.PS[pi][c * 64:(c + 1) * 64, :],
                        scalar=gcol[c * 64:(c + 1) * 64, :], in1=G[c * 64:(c + 1) * 64, :],
                        op0=ALU.mult, op1=ALU.mult),
                        reads=[self.r_ps[pi], self.r_scrg[i % 2], self.r_prep, self.r_const], writes=[rdst[c]])

            if h == 0:
                def cb(t):
                    if t >= 4 and t % 2 == 0:
                        i = 2 * (t // 4 - 1) + (t % 4) // 2
                        proj_mm(i)
                        norm_rest(i)
                self.rmsnorm_to_xnt(L, 0, cb=cb, stats_first=True)
                for i in (6, 7):
                    proj_mm(i)
                    norm_rest(i)
            else:
                proj_mm(0)
                for i in range(8):
                    if i + 1 < 8:
                        proj_mm(i + 1)
                    if 1 <= i <= 4:
                        step()
                    norm_rest(i)
            for tg in range(4):
                pi = PJ[tg % 4]

                def mmv(e):
                    last = None
                    for tt in range(4):
                        t = tg * 4 + tt
                        for kc in range(KC):
                            last = e.matmul(self.PS[pi][:, tt * 128:(tt + 1) * 128],
                                            lhsT=self.XNT[:, kc, t * 128:(t + 1) * 128],
                                            rhs=w[:, kc, 256:384], start=(kc == 0), stop=(kc == KC - 1))
                    return last
                ctx.op("pe", mmv, reads=[self.r_slab[bw], self.r_xnt[tg]], writes=[self.r_ps[pi]])
                ctx.op("dve", lambda e: e.tensor_copy(
                    out=VA[:, tg * 4:(tg + 1) * 4, 0:128],
                    in_=self.PS[pi][:].rearrange("p (t c) -> p t c", t=4)),
                    reads=[self.r_ps[pi]], writes=[r_va])
            self.w_rel(bw)
            for qg in range(4):
                items = [(c, kb) for c in range(2) for kb in range(4 * qg + 4)]

                def emit_S(idx):
                    c, kb = items[idx]
                    jj = kb - 4 * qg
                    col0 = max(jj, 0) * 128
                    pi = SBK[idx % 2]
                    ctx.op("pe", lambda e: e.matmul(
                        self.PS[pi][:, col0:512], lhsT=KA[c][:, kb * 128:(kb + 1) * 128],
                        rhs=QA[c][:, qg * 512 + col0:(qg + 1) * 512], start=True, stop=True),
                        reads=[r_ka[c], r_qa[c]], writes=[self.r_ps[pi]])

                emit_S(0)
                emit_S(1)
                for idx, (c, kb) in enumerate(items):
                    jj = kb - 4 * qg
                    col0 = max(jj, 0) * 128
                    pi = SBK[idx % 2]
                    pb = 2 + idx % 3
                    P = self.SCRB[pb]
                    ctx.op("act", lambda e: e.activation(out=P[:, col0:512], in_=self.PS[pi][:, col0:512],
                                                         func=AF.Exp, bias=-CSHIFT, scale=1.0),
                           reads=[self.r_ps[pi]], writes=[self.r_scrb[pb]])
                    if jj >= 0:
                        ctx.op("dve", lambda e: e.tensor_tensor(out=P[:, col0:col0 + 128], in0=P[:, col0:col0 + 128],
                                                                in1=self.tri, op=ALU.mult),
                               reads=[self.r_const], writes=[self.r_scrb[pb]])
                    tq0 = max(jj, 0)
                    if idx + 2 < len(items):
                        emit_S(idx + 2)

                    def av(e):
                        last = None
                        for tq in range(tq0, 4):
                            last = e.matmul(self.PS[OB[tq]][:, c * 129:(c + 1) * 129],
                                            lhsT=P[:, tq * 128:(tq + 1) * 128], rhs=VA[:, kb, :],
                                            start=(kb == 0), stop=(kb == 4 * qg + tq))
                        return last
                    ctx.op("pe", av, reads=[self.r_scrb[pb], r_va], writes=[self.r_ps[OB[tq]] for tq in range(tq0, 4)])
                    if jj < -1:
                        fill(1)
                    if idx in (2, 5, 8, 11):
                        step()
                OSB = [self.SCRF[tq] for tq in range(4)]
                r_osb = [self.r_scrf[tq] for tq in range(4)]
                for tq in range(4):
                    ctx.op("dve", lambda e: e.tensor_copy(out=OSB[tq][:, 0:258], in_=self.PS[OB[tq]][:, 0:258]),
                           reads=[self.r_ps[OB[tq]]], writes=[r_osb[tq]])
                st0 = 32
                r_st = self.r_stat[32]
                ssq = self.STAT[:, st0 + 12: st0 + 16]
                r_ss = self.r_stat[33]
                ON = self.SCRB[5]
                r_on = self.r_scrb[5]

                def step1(OSB=OSB, r_osb=r_osb):
                    for tq in range(4):
                        rden = self.STAT[:, st0 + tq * 2: st0 + tq * 2 + 2]
                        ctx.op("dve", lambda e: e.reciprocal(
                            out=rden, in_=OSB[tq][:, 0:258].rearrange("p (c k) -> p c k", c=2)[:, :, 128]),
                            reads=[r_osb[tq]], writes=[r_st])
                    for tq in range(4):
                        rden = self.STAT[:, st0 + tq * 2: st0 + tq * 2 + 2]
                        nr1 = self.STAT[:, st0 + 8 + tq: st0 + 9 + tq]
                        ctx.op("dve", lambda e: e.tensor_tensor(out=nr1, in0=rden[:, 1:2], in1=neglam, op=ALU.mult),
                               reads=[self.r_prep], writes=[r_st])
                    for tq in range(4):
                        rden = self.STAT[:, st0 + tq * 2: st0 + tq * 2 + 2]
                        nr1 = self.STAT[:, st0 + 8 + tq: st0 + 9 + tq]
                        ctx.op("dve", lambda e: e.tensor_scalar(out=OSB[tq][:, 129:257], in0=OSB[tq][:, 129:257],
                                                                scalar1=nr1, scalar2=None, op0=ALU.mult),
                               reads=[r_st], writes=[r_osb[tq]])
                        ctx.op("dve", lambda e: e.scalar_tensor_tensor(
                            out=OSB[tq][:, 0:128], in0=OSB[tq][:, 0:128], scalar=rden[:, 0:1], in1=OSB[tq][:, 129:257],
                            op0=ALU.mult, op1=ALU.add),
                            reads=[r_st], writes=[r_osb[tq]])

                def step2(OSB=OSB, r_osb=r_osb):
                    for tq in range(4):
                        ctx.op("act", lambda e: e.activation(out=OSB[tq][:, 258:386], in_=OSB[tq][:, 0:128],
                                                             func=AF.Square, accum_out=ssq[:, tq:tq + 1]),
                               writes=[r_osb[tq], r_ss])
                    ctx.op("act", lambda e: e.activation(out=ssq, in_=ssq, func=AF.Ln, scale=1.0 / 128, bias=EPS),
                           writes=[r_ss])
                    ctx.op("act", lambda e: e.activation(out=ssq, in_=ssq, func=AF.Exp, scale=-0.5),
                           writes=[r_ss])

                def step3(OSB=OSB, r_osb=r_osb):
                    for tq in range(4):
                        ctx.op("dve", lambda e: e.tensor_scalar(out=ON[:, tq * 128:(tq + 1) * 128], in0=OSB[tq][:, 0:128],
                                                                 scalar1=ssq[:, tq:tq + 1], scalar2=None, op0=ALU.mult),
                               reads=[r_osb[tq], r_ss], writes=[r_on])

                def post_pe(qg=qg, ON=ON, r_on=r_on, hb=hb, h=h, bo=bo):
                    drain(h - 2)
                    def tr(e):
                        last = None
                        for tq in range(4):
                            last = e.transpose(self.psb(TRB)[:, tq * 128:(tq + 1) * 128],
                                               ON[:, tq * 128:(tq + 1) * 128], self.ident)
                        return last
                    ctx.op("pe", tr, reads=[r_on, self.r_const], writes=[self.r_ps[TRB]])
                    ctx.op("dve", lambda e: e.tensor_scalar(out=OT[hb][:, qg * 512:(qg + 1) * 512],
                                                            in0=self.psb(TRB)[:, 0:512], scalar1=gsub, scalar2=None,
                                                            op0=ALU.mult),
                           reads=[self.r_ps[TRB], self.r_prep], writes=[r_ot[hb]])
                    if qg == 3:
                        wo = self.SLAB[bo]
                        for t in range(NT):
                            for half in range(2):
                                def wo_fill(t=t, half=half, last=(t == NT - 1 and half == 1)):
                                    wb = WOB + half
                                    ctx.op("pe", lambda e: e.matmul(
                                        self.PS[wb][:], lhsT=OT[hb][:, t * 128:(t + 1) * 128],
                                        rhs=wo[:, half * 512:(half + 1) * 512], start=True, stop=True),
                                        reads=[r_ot[hb], self.r_slab[bo]], writes=[self.r_ps[wb]])
                                    xs = self.X[:, t, half * 512:(half + 1) * 512]
                                    ctx.op("dve", lambda e: e.tensor_tensor(out=xs, in0=self.PS[wb][:], in1=xs,
                                                                            op=ALU.add),
                                           reads=[self.r_ps[wb]], writes=[self.r_x[t]])
                                    if last:
                                        self.w_rel(bo)
                                filler.append((h, wo_fill))
                assert not pending
                pending.extend([step1, step2, step3, post_pe])
        flush_post()
        drain(A_HEADS)
        assert not filler

    def retention(self, s, L):
        ctx = self.ctx
        j = L // 2
        A = self.ARENA
        QT = A[:, 0:4096].rearrange("p (k t) -> p k t", k=2)
        KT = A[:, 4096:8192].rearrange("p (k t) -> p k t", k=2)
        SG = A[:, 8192:16384].rearrange("p (n c) -> p n c", n=NT)
        r_qt, r_kt, r_sg = Res("qt"), Res("kt"), Res("sg")
        ggn = self.PFt[:, PF_B + j * 4: PF_B + j * 4 + 4]
        ggnb = ggn.unsqueeze(2).broadcast_to([128, 4, 128])
        TRB = 7
        for h in range(R_HEADS):
            bqk = self.w_get(("rqk", s, L, h))
            bv = self.w_get(("rv", s, L, h))
            bg = self.w_get(("rg", s, L, h))
            bo = self.w_get(("rwo", s, L, h))
            wqk = self.slab3(bqk, 512, 8)
            wv = self.slab3(bv, 512, 8)
            wg = self.slab3(bg, 512, 8)
            wo = self.slab3(bo, 1024, 4)
            dtab = self.CFt[:, CF_DT + h * 128: CF_DT + (h + 1) * 128]
            qdec = self.CFt[:, CF_QDEC + h: CF_QDEC + h + 1]
            kdec = self.CFt[:, CF_KDEC + h: CF_KDEC + h + 1]
            cdec = self.chunk_dec[h]
            def qk_unit(qk, dc, g):
                dst, rd = (QT, r_qt) if qk == 0 else (KT, r_kt)
                if True:
                    if True:
                        pi = self.next_ps(6)

                        def mm(e):
                            last = None
                            for kc in range(KC):
                                last = e.matmul(self.PS[pi][:],
                                                lhsT=wqk[:, kc, qk * 256 + dc * 128: qk * 256 + (dc + 1) * 128],
                                                rhs=self.XNT[:, kc, g * 512:(g + 1) * 512],
                                                start=(kc == 0), stop=(kc == KC - 1))
                            return last
                        ctx.op("pe", mm, reads=[self.r_slab[bqk], self.r_xnt[g]], writes=[self.r_ps[pi]])
                        if (dc * 4 + g) % 2 == 0:
                            ctx.op("act", lambda e: e.activation(out=dst[:, dc, g * 512:(g + 1) * 512],
                                                                 in_=self.PS[pi][:], func=AF.Copy),
                                   reads=[self.r_ps[pi]], writes=[rd])
                        else:
                            ctx.op("dve", lambda e: e.tensor_copy(out=dst[:, dc, g * 512:(g + 1) * 512],
                                                                  in_=self.PS[pi][:]),
                                   reads=[self.r_ps[pi]], writes=[rd])
            units = [(qk, dc) for qk in range(2) for dc in range(2)]
            if h == 0:
                self.rmsnorm_to_xnt(L, 0, cb=lambda t: qk_unit(*units[t % 4], t // 4 - 1) if t >= 4 else None)
                for u in units:
                    qk_unit(*u, 3)
            else:
                for qk, dc in units:
                    for g in range(4):
                        qk_unit(qk, dc, g)
            for n in range(NT):
                pg = self.next_ps(7)

                def mmg(e):
                    last = None
                    for kc in range(KC):
                        last = e.matmul(self.PS[pg][:], lhsT=self.XNT[:, kc, n * 128:(n + 1) * 128], rhs=wg[:, kc, :],
                                        start=(kc == 0), stop=(kc == KC - 1))
                    return last
                ctx.op("pe", mmg, reads=[self.r_slab[bg], self.r_xnt[n // 4]], writes=[self.r_ps[pg]])
                ctx.op("act", lambda e: e.activation(out=SG[:, n, :], in_=self.PS[pg][:], func=AF.Silu),
                       reads=[self.r_ps[pg]], writes=[r_sg])
            self.w_rel(bg)

            def stage_a(n):
                tok = slice(n * 128, (n + 1) * 128)
                pv = self.next_ps(7)

                def mmv(e):
                    last = None
                    for kc in range(KC):
                        last = e.matmul(self.PS[pv][:], lhsT=self.XNT[:, kc, tok], rhs=wv[:, kc, :],
                                        start=(kc == 0), stop=(kc == KC - 1))
                    return last
                ctx.op("pe", mmv, reads=[self.r_slab[bv], self.r_xnt[n // 4]], writes=[self.r_ps[pv]])
                vb = n % 2
                ctx.op("act", lambda e: e.activation(out=self.SCRB[vb][:], in_=self.PS[pv][:], func=AF.Copy),
                       reads=[self.r_ps[pv]], writes=[self.r_scrb[vb]])
                if n < NT - 1:
                    KD = self.SCRB[2 + n % 2]
                    pk = self.next_ps(7)

                    def mmk(e):
                        last = None
                        for kc in range(KC):
                            last = e.matmul(self.PS[pk][:, 0:256], lhsT=self.XNT[:, kc, tok],
                                            rhs=wqk[:, kc, 256:512], start=(kc == 0), stop=(kc == KC - 1))
                        return last
                    ctx.op("pe", mmk, reads=[self.r_slab[bqk], self.r_xnt[n // 4]], writes=[self.r_ps[pk]])
                    ctx.op("act", lambda e: e.activation(out=KD[:, 0:256], in_=self.PS[pk][:, 0:256], func=AF.Copy,
                                                         scale=kdec),
                           reads=[self.r_ps[pk], self.r_const], writes=[self.r_scrb[2 + n % 2]])
                psc = self.next_ps(7)

                def mms(e):
                    last = None
                    for dc in range(2):
                        last = e.matmul(self.PS[psc][:, 0:128], lhsT=KT[:, dc, tok], rhs=QT[:, dc, tok],
                                        start=(dc == 0), stop=(dc == 1))
                    return last
                ctx.op("pe", mms, reads=[r_kt, r_qt], writes=[self.r_ps[psc]])
                AM = self.SCRB[4 + n % 2]
                ctx.op("dve", lambda e: e.tensor_tensor(out=AM[:, 0:128], in0=self.PS[psc][:, 0:128], in1=dtab,
                                                        op=ALU.mult),
                       reads=[self.r_ps[psc], self.r_const], writes=[self.r_scrb[4 + n % 2]])

            ybufs = {}

            def stage_b(n):
                tok = slice(n * 128, (n + 1) * 128)
                vb = n % 2
                V = self.SCRB[vb]
                KD = self.SCRB[2 + n % 2]
                AM = self.SCRB[4 + n % 2]
                pacc = self.next_ps(7)

                def mma(e):
                    last = e.matmul(self.PS[pacc][:], lhsT=AM[:, 0:128], rhs=V[:], start=True, stop=(n == 0))
                    if n > 0:
                        for dc in range(2):
                            last = e.matmul(self.PS[pacc][:], lhsT=QT[:, dc, tok], rhs=self.RB[:, dc, :],
                                            start=False, stop=(dc == 1))
                    return last
                ctx.op("pe", mma, reads=[self.r_scrb[4 + n % 2], self.r_scrb[vb], r_qt] +
                       ([self.r_rb] if n > 0 else []), writes=[self.r_ps[pacc]])
                st = 48 + (n % 2) * 4
                ssq = self.STAT[:, st:st + 1]
                sc2 = self.STAT[:, st + 1:st + 2]
                r_st = self.r_stat[48 + n % 2]
                ctx.op("act", lambda e: e.activation(out=self.JUNK[:, 0:512], in_=self.PS[pacc][:], func=AF.Square,
                                                     scale=qdec, accum_out=ssq),
                       reads=[self.r_ps[pacc], self.r_const], writes=[self.r_junk, r_st])
                ctx.op("act", lambda e: e.activation(out=ssq, in_=ssq, func=AF.Sqrt, scale=1.0 / 512, bias=EPS),
                       writes=[r_st])
                if n < NT - 1:
                    for dc in range(2):
                        pr = self.next_ps(7)
                        ctx.op("pe", lambda e: e.matmul(self.PS[pr][:], lhsT=KD[:, dc * 128:(dc + 1) * 128], rhs=V[:],
                                                        start=True, stop=True),
                               reads=[self.r_scrb[2 + n % 2], self.r_scrb[vb]], writes=[self.r_ps[pr]])
                        if n == 0:
                            ctx.op("dve", lambda e: e.tensor_copy(out=self.R32[:, dc, :], in_=self.PS[pr][:]),
                                   reads=[self.r_ps[pr]], writes=[self.r_r32])
                        else:
                            ctx.op("dve", lambda e: e.scalar_tensor_tensor(
                                out=self.R32[:, dc, :], in0=self.R32[:, dc, :], scalar=cdec, in1=self.PS[pr][:],
                                op0=ALU.mult, op1=ALU.add),
                                reads=[self.r_ps[pr]], writes=[self.r_r32])
                ctx.op("dve", lambda e: e.reciprocal(out=ssq, in_=ssq), writes=[r_st])
                ctx.op("dve", lambda e: e.tensor_scalar(out=sc2, in0=ssq, scalar1=qdec, scalar2=None, op0=ALU.mult),
                       reads=[self.r_const], writes=[r_st])
                YB = self.SCRF[2 + n % 2]
                r_yb = self.r_scrf[2 + n % 2]
                Y = YB[:].bitcast(BF16)[:, 0:512]
                ctx.op("dve", lambda e: e.scalar_tensor_tensor(out=Y, in0=self.PS[pacc][:], scalar=sc2,
                                                               in1=SG[:, n, :], op0=ALU.mult, op1=ALU.mult),
                       reads=[self.r_ps[pacc], r_sg, r_st], writes=[r_yb])

                if n < NT - 1:
                    ctx.op("act", lambda e: e.activation(out=self.RB[:, 0, :], in_=self.R32[:, 0, :], func=AF.Copy),
                           reads=[self.r_r32], writes=[self.r_rb])
                    ctx.op("act", lambda e: e.activation(out=self.RB[:, 1, :], in_=self.R32[:, 1, :], func=AF.Copy),
                           reads=[self.r_r32], writes=[self.r_rb])

            def stage_c(n):
                YB = self.SCRF[2 + n % 2]
                r_yb = self.r_scrf[2 + n % 2]
                Y = YB[:].bitcast(BF16)[:, 0:512]
                YT = YB[:].bitcast(BF16)[:, 512:1024]

                def tr(e):
                    last = None
                    for ec in range(4):
                        last = e.transpose(self.psb(TRB)[:, ec * 128:(ec + 1) * 128], Y[:, ec * 128:(ec + 1) * 128],
                                           self.ident)
                    return last
                ctx.op("pe", tr, reads=[r_yb, self.r_const], writes=[self.r_ps[TRB]])
                ctx.op("dve", lambda e: e.tensor_tensor(
                    out=YT.rearrange("p (k c) -> p k c", k=4),
                    in0=self.psb(TRB)[:, 0:512].rearrange("p (k c) -> p k c", k=4), in1=ggnb, op=ALU.mult),
                    reads=[self.r_ps[TRB], self.r_const], writes=[r_yb])
                for half in range(2):
                    po = self.next_ps(7)

                    def mmo(e):
                        last = None
                        for ec in range(4):
                            last = e.matmul(self.PS[po][:], lhsT=YT[:, ec * 128:(ec + 1) * 128],
                                            rhs=wo[:, ec, half * 512:(half + 1) * 512],
                                            start=(ec == 0), stop=(ec == 3))
                        return last
                    ctx.op("pe", mmo, reads=[r_yb, self.r_slab[bo]], writes=[self.r_ps[po]])
                    xs = self.X[:, n, half * 512:(half + 1) * 512]
                    ctx.op("dve", lambda e: e.tensor_tensor(out=xs, in0=self.PS[po][:], in1=xs, op=ALU.add),
                           reads=[self.r_ps[po]], writes=[self.r_x[n]])

            stage_a(0)
            for n in range(NT):
                if n + 1 < NT:
                    stage_a(n + 1)
                if n >= 1:
                    stage_c(n - 1)
                stage_b(n)
            stage_c(NT - 1)
            for b in (bqk, bv, bo):
                self.w_rel(b)

    def prologue(self):
        ctx = self.ctx
        ctx.dma("sp", [(self.PFt[:], self.d_pf), (self.CFt[:], self.d_cf), (self.CBt[:], self.d_cb)],
                self.s_c, writes=[self.r_const])
        for j in range(2):
            L = 2 * j
            lam_init = 0.8 - 0.6 * math.exp(-0.3 * L)
            pa = PF_A + j * PF_A_STRIDE
            ctx.op("dve", lambda e: e.tensor_scalar(out=self.PREP[:, 2 * j:2 * j + 1], in0=self.PFt[:, pa:pa + 1],
                                                    scalar1=0.125, scalar2=None, op0=ALU.mult),
                   reads=[self.r_const], writes=[self.r_prep])
            ctx.op("dve", lambda e: e.tensor_scalar(out=self.PREP[:, 4 + j:5 + j], in0=self.PFt[:, pa + 2:pa + 3],
                                                    scalar1=1.0 - lam_init, scalar2=None, op0=ALU.mult),
                   reads=[self.r_const], writes=[self.r_prep])
            lam = self.PFt[:, pa + 3:pa + 259]
            r_l = Res("lam")
            for i in range(2):
                ctx.op("dve", lambda e: e.tensor_tensor(out=self.LAMT[:, i * 64:(i + 1) * 64],
                                                        in0=lam[:, i * 128:i * 128 + 64],
                                                        in1=lam[:, i * 128 + 64:i * 128 + 128], op=ALU.mult),
                       reads=[self.r_const], writes=[r_l])
                ctx.op("act", lambda e: e.activation(out=self.JUNK[:, 0:64], in_=self.LAMT[:, i * 64:(i + 1) * 64],
                                                     func=AF.Copy, accum_out=self.PREP[:, 12 + i:13 + i]),
                       reads=[r_l], writes=[self.r_junk, self.r_prep])
                ctx.op("act", lambda e: e.activation(out=self.PREP[:, 12 + i:13 + i], in_=self.PREP[:, 12 + i:13 + i],
                                                     func=AF.Exp),
                       writes=[self.r_prep])
            ctx.op("dve", lambda e: e.tensor_tensor(out=self.PREP[:, 8 + j:9 + j], in0=self.PREP[:, 13:14],
                                                    in1=self.PREP[:, 12:13], op=ALU.subtract),
                   writes=[self.r_prep])
            ctx.op("dve", lambda e: e.tensor_scalar(out=self.PREP[:, 8 + j:9 + j], in0=self.PREP[:, 8 + j:9 + j],
                                                    scalar1=-lam_init, scalar2=None, op0=ALU.add),
                   writes=[self.r_prep])

    def x_load(self, s, t):
        self.ctx.dma("sp", [(self.X[:, t, :], self.d_x[s, t * 128:(t + 1) * 128, :])], self.s_xl[t],
                     writes=[self.r_x[t]])

    def x_store(self, s, t):
        self.ctx.dma("sp", [(self.d_y[s, t * 128:(t + 1) * 128, :], self.X[:, t, :])], self.s_xs[t],
                     reads=[self.r_x[t]])

    def build(self):
        nc = self.nc
        self._dram()
        with ExitStack() as es:
            self._alloc(es)
            self.ctx = Ctx(nc, es, self.sig)
            ctx = self.ctx
            self.w_init()
            self.prologue()
            for t in range(NT):
                self.x_load(0, t)
            for s in range(self.nseq):
                for L in self.layers:
                    if L % 2 == 0:
                        self.attention(s, L)
                    else:
                        self.retention(s, L)
                    ctx.barrier()
                    self.mlp(s, L)
                    ctx.barrier()
                    self.ple(s, L, final=(L == self.layers[-1]))
                    ctx.barrier()
            for t in range(NT):
                nc.sync.wait_ge(self.s_xs[t].h, self.s_xs[t].n)
            assert self.w_next_get == len(self.plan)
        return nc


_WEIGHT_KEYS = ("a_w_qkv", "a_w_o", "r_w_in", "r_w_out", "mlp_w1", "mlp_w2", "pe_w_up", "pe_w_gate")


def _run(inputs, x_full, layers, core_ids=None):
    cb, cf, augb, _ = _const_tables()
    pf = _pack_params(inputs)
    dry = Builder(layers)
    dry.build()
    nc = Builder(layers, sig=dry.ctx.log).build()
    n = N_CORES if core_ids is None else len(core_ids)
    w = {k: np.ascontiguousarray(np.asarray(inputs[k], np.float32)) for k in _WEIGHT_KEYS}
    p = np.asarray(inputs["p"], np.float32)
    in_maps = []
    for c in range(n):
        b0 = c * SEQ_PER_CORE
        m = dict(w)
        m["x"] = np.ascontiguousarray(x_full[b0:b0 + SEQ_PER_CORE])
        m["p"] = np.ascontiguousarray(p[:, b0:b0 + SEQ_PER_CORE])
        m["params_f32"] = pf
        m["consts_f32"] = cf
        m["consts_bf16"] = cb
        m["aug_bf16"] = augb
        in_maps.append(m)
    res = run_bass_kernel_spmd(nc, in_maps, core_ids=list(range(n)), **({"trace": True} if TRACE else {}))
    if TRACE:
        print("exec_time_ns", res.exec_time_ns)
    return np.concatenate([np.asarray(r["y"], np.float32) for r in res.results], axis=0)


FUSED = True
TRACE = False


def kernel(**inputs):
    x = np.ascontiguousarray(np.asarray(inputs["x"], np.float32))
    if FUSED:
        return _run(inputs, x, [0, 1, 2, 3])
    for L in range(DEPTH):
        x = _run(inputs, x, [L])
    return x
```

```python
import math
from contextlib import ExitStack

import numpy as np
import ml_dtypes

import concourse.bass as bass
import concourse.mybir as mybir
from concourse.bass_utils import run_bass_kernel_spmd

F32 = mybir.dt.float32
BF16 = mybir.dt.bfloat16
AF = mybir.ActivationFunctionType
ALU = mybir.AluOpType

N_CORES = 8
SEQ_PER_CORE = 2
S = 2048
D = 1024
NT = S // 128
KC = D // 128
DEPTH = 4
DFF = 4096
PLE = 256
EPS = 1e-6
A_HEADS = 8
R_HEADS = 4
NSLAB = 5
SLAB_COLS = 4096
CSHIFT = 4.0


class Sem:
    def __init__(self, nc, es, name):
        self.h = es.enter_context(nc.semaphore(name))
        self.n = 0
        self.name = name
        self.sig = 0
        self.rank = {}


class Res:
    __slots__ = ("name", "w", "r")

    def __init__(self, name):
        self.name = name
        self.w = None
        self.r = {}


class EngQ:
    def __init__(self, nc, es, eng, name, is_pe=False, compute=True):
        self.e = eng
        self.name = name
        self.is_pe = is_pe
        self.sem = Sem(nc, es, "q_" + name) if compute else None
        if self.sem is not None:
            self.sem.is_dma = False
        self.waited = {}
        self.log = None

    def wait(self, toks):
        best = {}
        for sem, val in toks:
            if best.get(sem, 0) < val:
                best[sem] = val
        for sem, val in best.items():
            if self.waited.get(sem, 0) >= val:
                continue
            if self.is_pe and sem is self.sem:
                continue
            if sem.is_dma:
                self.e.wait_ge(sem.h, val)
            else:
                self.log.setdefault(sem.name, set()).add(val)
                self.e.wait_ge(sem.h, sem.rank.get(val, val))
            self.waited[sem] = val


class Ctx:
    def __init__(self, nc, es, sig=None):
        self.nc = nc
        self.sig = sig
        self.log = {}
        self.q = {
            "pe": EngQ(nc, es, nc.tensor, "pe", is_pe=True),
            "act": EngQ(nc, es, nc.scalar, "act"),
            "dve": EngQ(nc, es, nc.vector, "dve"),
            "pool": EngQ(nc, es, nc.gpsimd, "pool"),
            "sp": EngQ(nc, es, nc.sync, "sp", compute=False),
        }
        for E in self.q.values():
            E.log = self.log

    @staticmethod
    def _deps(reads, writes):
        toks = []
        for r in reads:
            if r.w is not None:
                toks.append(r.w)
        for w in writes:
            if w.w is not None:
                toks.append(w.w)
            toks.extend(w.r.items())
        return toks

    def op(self, eng, fn, reads=(), writes=()):
        E = self.q[eng]
        E.wait(self._deps(reads, writes))
        inst = fn(E.e)
        E.sem.n += 1
        if self.sig is None or E.sem.n in self.sig.get(E.sem.name, ()):
            inst.then_inc(E.sem.h, 1)
            E.sem.sig += 1
            if self.sig is not None:
                E.sem.rank[E.sem.n] = E.sem.sig
        tok = (E.sem, E.sem.n)
        for r in reads:
            r.r[E.sem] = E.sem.n
        for w in writes:
            w.w = tok
            w.r = {}
        return tok

    def dma(self, eng, pairs, sem, reads=(), writes=()):
        E = self.q[eng]
        E.wait(self._deps(reads, writes))
        for out, in_ in pairs:
            E.e.dma_start(out=out, in_=in_).then_inc(sem.h, 16)
            sem.n += 16
        tok = (sem, sem.n)
        for r in reads:
            r.r[sem] = sem.n
        for w in writes:
            w.w = tok
            w.r = {}
        return tok

    def barrier(self):
        comp = [self.q[k] for k in ("pe", "act", "dve", "pool")]
        for E in self.q.values():
            E.wait([(F.sem, F.sem.n) for F in comp if F is not E and F.sem.n > 0])


def _const_tables():
    bf = ml_dtypes.bfloat16
    idx = np.arange(128)
    ident = np.eye(128, dtype=np.float32)
    tri = (idx[None, :] >= idx[:, None]).astype(np.float32)
    bd = np.zeros((128, 128), np.float32)
    bd[:64, :64] = 1.0
    bd[64:, 64:] = 1.0
    cb = np.concatenate([ident, tri, bd], axis=1).astype(bf)

    gam = 1.0 - 2.0 ** (-5.0 - np.arange(R_HEADS, dtype=np.float64))
    j = idx.astype(np.float64)
    dt = np.zeros((128, R_HEADS, 128), np.float64)
    for h in range(R_HEADS):
        dt[:, h, :] = (gam[h] ** (-(j[:, None] + 1.0))) / 16.0 * (idx[None, :] >= idx[:, None])
    qdec = gam[None, :] ** (j[:, None] + 1.0)
    kdec16 = gam[None, :] ** (127.0 - j[:, None]) / 16.0
    cf = np.concatenate([dt.reshape(128, -1), qdec, kdec16], axis=1).astype(np.float32)
    chunk_dec = [float(g ** 128.0) for g in gam]

    pos = np.arange(S)
    blk = (pos // 128).astype(np.float64)
    inn = (pos % 128).astype(np.float64)
    qaug = np.zeros((A_HEADS, 4, S), np.float64)
    kaug = np.zeros((A_HEADS, 4, S), np.float64)
    for h in range(A_HEADS):
        sl = 2.0 ** (-8.0 * (h + 1) / A_HEADS)
        qaug[h, 0] = -sl * 128.0 * blk
        qaug[h, 1] = -sl * inn
        qaug[h, 2] = 1.0
        qaug[h, 3] = 1.0
        kaug[h, 0] = 1.0
        kaug[h, 1] = 1.0
        kaug[h, 2] = sl * 128.0 * blk
        kaug[h, 3] = sl * inn
    aug = np.stack([qaug, kaug], 0).astype(np.float32)
    augb = aug.astype(bf)
    assert np.array_equal(augb.astype(np.float32), aug)
    return cb, cf, augb, chunk_dec


CF_DT, CF_QDEC, CF_KDEC = 0, 512, 516
NCF = 520
PF_G = 0
PF_A = 96
PF_A_STRIDE = 259
PF_B = PF_A + 2 * PF_A_STRIDE
NPF = PF_B + 2 * 4


def _pack_params(inp):
    pf = np.zeros((128, NPF), np.float32)
    norms = (inp["norm_mix"], inp["norm_mlp"], inp["norm_pe"])
    for L in range(DEPTH):
        for w in range(3):
            c = PF_G + (L * 3 + w) * 8
            pf[:, c:c + 8] = np.asarray(norms[w][L], np.float32).reshape(8, 128).T
    for j in range(2):
        c = PF_A + j * PF_A_STRIDE
        pf[:, c] = np.tile(np.asarray(inp["a_g_q"][j], np.float32), 2)
        pf[:, c + 1] = np.tile(np.asarray(inp["a_g_k"][j], np.float32), 2)
        pf[:, c + 2] = np.asarray(inp["a_g_sub"][j], np.float32)
        lam = np.concatenate([np.asarray(inp[k][j], np.float32) for k in
                              ("a_lam_q1", "a_lam_k1", "a_lam_q2", "a_lam_k2")])
        pf[:, c + 3:c + 259] = lam[None, :]
    for j in range(2):
        c = PF_B + j * 4
        pf[:, c:c + 4] = np.asarray(inp["r_g_gn"][j], np.float32).reshape(4, 128).T
    return pf


class Builder:
    def __init__(self, layers, nseq=SEQ_PER_CORE, sig=None):
        self.layers = list(layers)
        self.nseq = nseq
        self.sig = sig
        self.nc = bass.Bass("TRN2", target_bir_lowering=False)
        _, _, _, self.chunk_dec = _const_tables()

    def _dram(self):
        nc = self.nc
        di = lambda n, shp, dt=F32: nc.dram_tensor(n, list(shp), dt, kind="ExternalInput").ap()
        self.d_x = di("x", [self.nseq, S, D])
        self.d_p = di("p", [DEPTH, self.nseq, S, PLE])
        self.d_wqkv = di("a_w_qkv", [2, D, 3 * D])
        self.d_wo = di("a_w_o", [2, D, D])
        self.d_win = di("r_w_in", [2, D, 6 * D])
        self.d_wout = di("r_w_out", [2, 2 * D, D])
        self.d_w1 = di("mlp_w1", [DEPTH, D, DFF])
        self.d_w2 = di("mlp_w2", [DEPTH, DFF, D])
        self.d_wup = di("pe_w_up", [DEPTH, PLE, D])
        self.d_wgate = di("pe_w_gate", [DEPTH, D, D])
        self.d_pf = di("params_f32", [128, NPF])
        self.d_cf = di("consts_f32", [128, NCF])
        self.d_cb = di("consts_bf16", [128, 384], BF16)
        self.d_aug = di("aug_bf16", [2, A_HEADS, 4, S], BF16)
        self.d_y = nc.dram_tensor("y", [self.nseq, S, D], F32, kind="ExternalOutput").ap()

    def _alloc(self, es):
        nc = self.nc
        sb = lambda n, shp, dt: es.enter_context(nc.sbuf_tensor(n, list(shp), dt))
        self.X = sb("X", [128, NT, D], F32)
        self.XNT = sb("XNT", [128, KC, S], BF16)
        self.ARENA = sb("ARENA", [128, 16384], BF16)
        self.SLAB = [sb(f"SLAB{i}", [128, SLAB_COLS], BF16) for i in range(NSLAB)]
        self.PFt = sb("PF", [128, NPF], F32)
        self.CFt = sb("CF", [128, NCF], F32)
        self.CBt = sb("CB", [128, 384], BF16)
        self.PREP = sb("PREP", [128, 16], F32)
        self.LAMT = sb("LAMT", [128, 128], F32)
        self.JUNK = sb("JUNK", [128, D], BF16)
        self.XNP = [sb(f"XNP{i}", [128, D], BF16) for i in range(2)]
        self.STAT = sb("STAT", [128, 64], F32)
        self.SCRF = [sb(f"SCRF{i}", [128, 512], F32) for i in range(4)]
        self.SCRB = [sb(f"SCRB{i}", [128, 512], BF16) for i in range(6)]
        self.R32 = sb("R32", [128, 2, 512], F32)
        self.RB = sb("RB", [128, 2, 512], BF16)
        ps = lambda n, shp, dt: es.enter_context(nc.psum_tensor(n, list(shp), dt))
        self.PS = [ps(f"PS{i}", [128, 512], F32) for i in range(8)]
        self.SCRG = [sb(f"SCRG{i}", [128, 512], F32) for i in range(2)]
        self.r_x = [Res(f"x{t}") for t in range(NT)]
        self.r_xnt = [Res(f"xnt{g}") for g in range(4)]
        self.r_slab = [Res(f"slab{i}") for i in range(NSLAB)]
        self.r_ps = [Res(f"ps{i}") for i in range(8)]
        self.r_scrg = [Res("scrg0"), Res("scrg1")]
        self.r_junk = Res("junk")
        self.r_xnp = [Res("xnp0"), Res("xnp1")]
        self.r_stat = [Res(f"stat{i}") for i in range(64)]
        self.r_scrf = [Res(f"scrf{i}") for i in range(4)]
        self.r_scrb = [Res(f"scrb{i}") for i in range(6)]
        self.r_const = Res("const")
        self.r_prep = Res("prep")
        self.r_r32 = Res("r32")
        self.r_rb = Res("rb")
        self.ps_rr = 0
        self.r_ht = [Res("ht0"), Res("ht1")]
        self.r_pbf = Res("pbf")
        self.p_issued = None
        self.s_slab = [Sem(nc, es, f"d_slab{i}") for i in range(NSLAB)]
        self.s_x = Sem(nc, es, "d_x")
        self.s_y = Sem(nc, es, "d_y")
        self.s_c = Sem(nc, es, "d_c")
        self.s_p = Sem(nc, es, "d_p")
        self.s_aug = Sem(nc, es, "d_aug")
        self.s_xl = [Sem(nc, es, f"d_xl{t}") for t in range(NT)]
        self.s_xs = [Sem(nc, es, f"d_xs{t}") for t in range(NT)]
        for sm in self.s_slab + [self.s_x, self.s_y, self.s_c, self.s_p, self.s_aug] + self.s_xl + self.s_xs:
            sm.is_dma = True
        self.ident = self.CBt[:, 0:128]
        self.tri = self.CBt[:, 128:256]
        self.bd = self.CBt[:, 256:384]

    def next_ps(self, n=6):
        i = self.ps_rr % n
        self.ps_rr = (i + 1) % n
        return i

    def psb(self, i):
        return self.PS[i][:].bitcast(BF16)

    def _weight_plan(self):
        plan = []
        for s in range(self.nseq):
            for L in self.layers:
                j = L // 2
                if L % 2 == 0:
                    w = self.d_wqkv[j].rearrange("(kc p) c -> p kc c", p=128)
                    for h in range(A_HEADS):
                        parts = []
                        for sec in range(3):
                            parts.append(((sec * 128, 128), w[:, :, sec * D + h * 128: sec * D + (h + 1) * 128]))
                        plan.append((("qkv", s, L, h), 384, parts))
                        plan.append((("wo", s, L, h), 1024, [((0, 1024), self.d_wo[j][h * 128:(h + 1) * 128, :])]))
                else:
                    w = self.d_win[j].rearrange("(kc p) c -> p kc c", p=128)
                    wo = self.d_wout[j].rearrange("(ec p) c -> p ec c", p=128)
                    for h in range(R_HEADS):
                        plan.append((("rqk", s, L, h), 512, [((0, 256), w[:, :, h * 256:(h + 1) * 256]),
                                                              ((256, 256), w[:, :, D + h * 256: D + (h + 1) * 256])]))
                        plan.append((("rv", s, L, h), 512, [((0, 512), w[:, :, 2 * D + h * 512: 2 * D + (h + 1) * 512])]))
                        plan.append((("rg", s, L, h), 512, [((0, 512), w[:, :, 4 * D + h * 512: 4 * D + (h + 1) * 512])]))
                        plan.append((("rwo", s, L, h), 1024, [((0, 1024), wo[:, h * 4:(h + 1) * 4, :])]))
                w1 = self.d_w1[L].rearrange("(kc p) f -> p kc f", p=128)
                w2 = self.d_w2[L].rearrange("(fc p) d -> p fc d", p=128)
                for sl in range(8):
                    plan.append((("w1", s, L, sl), 512, [((0, 512), w1[:, :, sl * 512:(sl + 1) * 512])]))
                    plan.append((("w2", s, L, sl), 1024, [((0, 1024), w2[:, sl * 4:(sl + 1) * 4, :])]))
                wg = self.d_wgate[L].rearrange("(kc p) c -> p kc c", p=128)
                wu = self.d_wup[L].rearrange("(kc p) c -> p kc c", p=128)
                for half in range(2):
                    plan.append((("wg", s, L, half), 512, [((0, 512), wg[:, :, half * 512:(half + 1) * 512])]))
                plan.append((("wu", s, L, 0), 1024, [((0, 1024), wu)]))
        return plan

    def w_init(self):
        self.plan = self._weight_plan()
        self.w_next_issue = 0
        self.w_next_get = 0
        self.w_free = list(range(NSLAB))
        self.w_loc = {}

    def w_pump(self):
        ctx = self.ctx
        while self.w_free and self.w_next_issue < len(self.plan):
            key, rowlen, parts = self.plan[self.w_next_issue]
            b = self.w_free.pop(0)
            pairs = []
            for (c0, cn), dap in parts:
                n_outer = dap.shape[1] if len(dap.shape) == 3 else 1
                if len(dap.shape) == 3:
                    view = self.SLAB[b][:, 0:n_outer * rowlen].rearrange("p (k c) -> p k c", c=rowlen)[:, :, c0:c0 + cn]
                else:
                    view = self.SLAB[b][:, c0:c0 + cn]
                pairs.append((view, dap))
            ctx.dma("pool", pairs, self.s_slab[b], writes=[self.r_slab[b]])
            self.w_loc[self.w_next_issue] = b
            self.w_next_issue += 1

    def w_get(self, key):
        self.w_pump()
        i = self.w_next_get
        assert self.plan[i][0] == key, (self.plan[i][0], key)
        assert i in self.w_loc, "weight slab pool deadlock: " + str(key)
        self.w_next_get += 1
        b = self.w_loc.pop(i)
        return b

    def w_rel(self, b):
        self.w_free.append(b)
        self.w_pump()

    def slab3(self, b, rowlen, k):
        return self.SLAB[b][:, 0:k * rowlen].rearrange("p (k c) -> p k c", c=rowlen)

    def rmsnorm_to_xnt(self, L, which, cb=None, stats_first=False):
        ctx = self.ctx
        gcol = self.PFt[:, PF_G + (L * 3 + which) * 8: PF_G + (L * 3 + which) * 8 + 8]
        gb = gcol.unsqueeze(2).broadcast_to([128, 8, 128])

        def stats(t):
            ss = self.STAT[:, t:t + 1]
            rs = self.STAT[:, 16 + t:17 + t]
            r_ss, r_rs = self.r_stat[t], self.r_stat[16 + t]
            ctx.op("act", lambda e: e.activation(out=self.JUNK[:], in_=self.X[:, t, :], func=AF.Square,
                                                 accum_out=ss),
                   reads=[self.r_x[t]], writes=[self.r_junk, r_ss])
            ctx.op("act", lambda e: e.activation(out=rs, in_=ss, func=AF.Sqrt, scale=1.0 / D, bias=EPS),
                   reads=[r_ss], writes=[r_rs])

        def scale(t):
            b = t % 2
            rs = self.STAT[:, 16 + t:17 + t]
            r_rs = self.r_stat[16 + t]
            ctx.op("dve", lambda e: e.reciprocal(out=rs, in_=rs), writes=[r_rs])
            ctx.op("dve", lambda e: e.tensor_scalar(out=self.XNP[b][:], in0=self.X[:, t, :], scalar1=rs,
                                                    scalar2=None, op0=ALU.mult),
                   reads=[self.r_x[t], r_rs], writes=[self.r_xnp[b]])

        def transp(t):
            b = t % 2

            def tr(e):
                last = None
                for kc in range(KC):
                    last = e.transpose(self.psb(6 + b)[:, kc * 128:(kc + 1) * 128],
                                       self.XNP[b][:, kc * 128:(kc + 1) * 128], self.ident)
                return last
            ctx.op("pe", tr, reads=[self.r_xnp[b], self.r_const], writes=[self.r_ps[6 + b]])

        def evac(t):
            b = t % 2
            ctx.op("dve", lambda e: e.tensor_tensor(
                out=self.XNT[:, :, t * 128:(t + 1) * 128],
                in0=self.psb(6 + b).rearrange("p (k c) -> p k c", k=8), in1=gb, op=ALU.mult),
                reads=[self.r_ps[6 + b], self.r_const], writes=[self.r_xnt[t // 4]])

        if stats_first:
            for t in range(NT):
                stats(t)
        else:
            stats(0)
            stats(1)
        scale(0)
        for t in range(NT):
            if t + 2 < NT and not stats_first:
                stats(t + 2)
            transp(t)
            if t + 1 < NT:
                scale(t + 1)
            evac(t)
            if cb is not None:
                cb(t)

    def mlp(self, s, L):
        ctx = self.ctx
        HT = [self.ARENA[:, i * 8192:(i + 1) * 8192].rearrange("p (f t) -> p f t", f=4) for i in range(2)]
        r_ht = self.r_ht
        slabs = {}
        w1s = {}

        def up_begin(sl):
            w1s[sl] = self.w_get(("w1", s, L, sl))
            slabs[sl] = self.w_get(("w2", s, L, sl))

        def up_part(sl, order):
            hb = sl % 2
            b1 = w1s[sl]
            w1 = self.slab3(b1, 512, 8)
            for fc, g in order:
                if True:
                    pi = self.next_ps()

                    def mm(e):
                        last = None
                        for kc in range(KC):
                            last = e.matmul(self.PS[pi][:], lhsT=w1[:, kc, fc * 128:(fc + 1) * 128],
                                            rhs=self.XNT[:, kc, g * 512:(g + 1) * 512],
                                            start=(kc == 0), stop=(kc == KC - 1))
                        return last
                    ctx.op("pe", mm, reads=[self.r_slab[b1], self.r_xnt[g]], writes=[self.r_ps[pi]])
                    sf = (fc * 4 + g) % 4
                    ctx.op("act", lambda e: e.activation(out=self.SCRF[sf][:], in_=self.PS[pi][:], func=AF.Relu),
                           reads=[self.r_ps[pi]], writes=[self.r_scrf[sf]])
                    eng = "dve" if (fc * 4 + g) % 2 == 0 else "pool"
                    ctx.op(eng, lambda e: e.tensor_tensor(out=HT[hb][:, fc, g * 512:(g + 1) * 512],
                                                          in0=self.SCRF[sf][:], in1=self.SCRF[sf][:], op=ALU.mult),
                           reads=[self.r_scrf[sf]], writes=[r_ht[hb]])

        def up(sl):
            up_begin(sl)
            up_part(sl, [(fc, g) for fc in range(4) for g in range(4)])
            self.w_rel(w1s.pop(sl))

        def down(sl):
            hb = sl % 2
            b2 = slabs.pop(sl)
            w2 = self.slab3(b2, 1024, 4)
            for t in range(NT):
                for half in range(2):
                    pi = self.next_ps()

                    def mm(e):
                        last = None
                        for fc in range(4):
                            last = e.matmul(self.PS[pi][:], lhsT=HT[hb][:, fc, t * 128:(t + 1) * 128],
                                            rhs=w2[:, fc, half * 512:(half + 1) * 512],
                                            start=(fc == 0), stop=(fc == 3))
                        return last
                    ctx.op("pe", mm, reads=[self.r_slab[b2], r_ht[hb]], writes=[self.r_ps[pi]])
                    xs = self.X[:, t, half * 512:(half + 1) * 512]
                    ctx.op("dve", lambda e: e.tensor_tensor(out=xs, in0=self.PS[pi][:], in1=xs, op=ALU.add),
                           reads=[self.r_ps[pi]], writes=[self.r_x[t]])
            self.w_rel(b2)

        up_begin(0)
        self.rmsnorm_to_xnt(L, 1, cb=lambda t: up_part(0, [(t % 4, t // 4 - 1)]) if t >= 4 else None)
        up_part(0, [(fc, 3) for fc in range(4)])
        self.w_rel(w1s.pop(0))
        for sl in range(8):
            if sl + 1 < 8:
                up(sl + 1)
            down(sl)
            if sl == 6:
                self.issue_p(s, L)

    def ple(self, s, L, final=False):
        ctx = self.ctx
        PBF = self.ARENA[:, 0:4096].rearrange("p (t c) -> p t c", t=NT)
        PT = self.ARENA[:, 4096:8192].rearrange("p (k t) -> p k t", k=2)
        r_pbf, r_pt = self.r_pbf, Res("pt")
        if self.p_issued != (s, L):
            self.issue_p(s, L)
        for tg in range(4):
            b = tg % 2

            def tr(e):
                last = None
                for tt in range(4):
                    for c2 in range(2):
                        last = e.transpose(self.psb(6 + b)[:, (tt * 2 + c2) * 128:(tt * 2 + c2 + 1) * 128],
                                           PBF[:, tg * 4 + tt, c2 * 128:(c2 + 1) * 128], self.ident)
                return last
            ctx.op("pe", tr, reads=[r_pbf, self.r_const], writes=[self.r_ps[6 + b]])
            for c2 in range(2):
                ctx.op("dve", lambda e: e.tensor_copy(
                    out=PT[:, c2, tg * 512:(tg + 1) * 512].rearrange("p (t c) -> p t c", t=4),
                    in_=self.psb(6 + b).rearrange("p (t k c) -> p t k c", t=4, k=2)[:, :, c2, :]),
                    reads=[self.r_ps[6 + b]], writes=[r_pt])
        bg = [self.w_get(("wg", s, L, half)) for half in range(2)]
        bu = self.w_get(("wu", s, L, 0))
        wu = self.slab3(bu, 1024, 2)
        def tile_work(t):
            for half in range(2):
                wg = self.slab3(bg[half], 512, 8)
                pg = self.next_ps()

                def mmg(e):
                    last = None
                    for kc in range(KC):
                        last = e.matmul(self.PS[pg][:], lhsT=self.XNT[:, kc, t * 128:(t + 1) * 128],
                                        rhs=wg[:, kc, :], start=(kc == 0), stop=(kc == KC - 1))
                    return last
                ctx.op("pe", mmg, reads=[self.r_slab[bg[half]], self.r_xnt[t // 4]], writes=[self.r_ps[pg]])
                pu = self.next_ps()

                def mmu(e):
                    last = None
                    for c2 in range(2):
                        last = e.matmul(self.PS[pu][:], lhsT=PT[:, c2, t * 128:(t + 1) * 128],
                                        rhs=wu[:, c2, half * 512:(half + 1) * 512], start=(c2 == 0), stop=(c2 == 1))
                    return last
                ctx.op("pe", mmu, reads=[self.r_slab[bu], r_pt], writes=[self.r_ps[pu]])
                sf = (t * 2 + half) % 4
                ctx.op("act", lambda e: e.activation(out=self.SCRF[sf][:], in_=self.PS[pg][:], func=AF.Sigmoid),
                       reads=[self.r_ps[pg]], writes=[self.r_scrf[sf]])
                ctx.op("dve", lambda e: e.tensor_tensor(out=self.SCRF[sf][:], in0=self.PS[pu][:],
                                                        in1=self.SCRF[sf][:], op=ALU.mult),
                       reads=[self.r_ps[pu]], writes=[self.r_scrf[sf]])
                xs = self.X[:, t, half * 512:(half + 1) * 512]
                ctx.op("pool", lambda e: e.tensor_tensor(out=xs, in0=self.SCRF[sf][:], in1=xs, op=ALU.add),
                       reads=[self.r_scrf[sf]], writes=[self.r_x[t]])
            if final:
                self.x_store(s, t)
                if s + 1 < self.nseq and t >= 1:
                    self.x_load(s + 1, t - 1)
        self.rmsnorm_to_xnt(L, 2, cb=lambda t: tile_work(t - 4) if t >= 4 else None, stats_first=True)
        for t in range(NT - 4, NT):
            tile_work(t)
        if final and s + 1 < self.nseq:
            self.x_load(s + 1, NT - 1)
        for b in bg:
            self.w_rel(b)
        self.w_rel(bu)

    def attention(self, s, L):
        ctx = self.ctx
        j = L // 2
        A = self.ARENA
        QA = [A[:, 0:2048], A[:, 2048:4096]]
        KA = [A[:, 4096:6144], A[:, 6144:8192]]
        VA = A[:, 8192:8192 + NT * 129].rearrange("p (t c) -> p t c", c=129)
        OT = [A[:, 12288:14336], A[:, 14336:16384]]
        r_qa = [Res("qa0"), Res("qa1")]
        r_ka = [Res("ka0"), Res("ka1")]
        r_va = Res("va")
        r_ot = [Res("ot0"), Res("ot1")]
        ctx.op("dve", lambda e: e.memset(VA[:, :, 128:129], 1.0), writes=[r_va])
        for c in range(2):
            ctx.op("dve", lambda e: e.memset(QA[c][64:128, :], 0.0), writes=[r_qa[c]])
            ctx.op("dve", lambda e: e.memset(KA[c][64:128, :], 0.0), writes=[r_ka[c]])
        pa = PF_A + j * PF_A_STRIDE
        gq = self.PREP[:, 2 * j:2 * j + 1]
        gk = self.PFt[:, pa + 1:pa + 2]
        gsub = self.PREP[:, 4 + j:5 + j]
        neglam = self.PREP[:, 8 + j:9 + j]
        OB = [0, 1, 2, 3]
        SBK = [4, 5]
        PJ = [4, 5, 0, 1]
        SUMS = [2, 3]
        WOB = 6
        TRB = 7
        filler = []

        def fill(n):
            for _ in range(n):
                if filler:
                    filler.pop(0)[1]()

        def drain(upto_head):
            while filler and filler[0][0] <= upto_head:
                filler.pop(0)[1]()

        pending = []

        def step():
            if pending:
                pending.pop(0)()

        def flush_post():
            while pending:
                pending.pop(0)()

        for h in range(A_HEADS):
            hb = h % 2
            bw = self.w_get(("qkv", s, L, h))
            bo = self.w_get(("wo", s, L, h))
            w = self.slab3(bw, 384, 8)
            ctx.dma("sp", [(QA[0][64:68, :], self.d_aug[0, h]), (QA[1][64:68, :], self.d_aug[0, h]),
                           (KA[0][64:68, :], self.d_aug[1, h]), (KA[1][64:68, :], self.d_aug[1, h])],
                    self.s_aug, writes=r_qa + r_ka)
            tiles = [(qk, g) for g in range(4) for qk in range(2)] if h == 0 else \
                    [(qk, g) for qk in range(2) for g in range(4)]

            def proj_mm(i):
                qk, g = tiles[i]
                pi = PJ[i % 4]

                def mm(e):
                    last = None
                    for kc in range(KC):
                        last = e.matmul(self.PS[pi][:], lhsT=w[:, kc, qk * 128:(qk + 1) * 128],
                                        rhs=self.XNT[:, kc, g * 512:(g + 1) * 512],
                                        start=(kc == 0), stop=(kc == KC - 1))
                    return last
                ctx.op("pe", mm, reads=[self.r_slab[bw], self.r_xnt[g]], writes=[self.r_ps[pi]])
                ctx.op("act", lambda e: e.activation(out=self.SCRB[i % 2][:], in_=self.PS[pi][:], func=AF.Square),
                       reads=[self.r_ps[pi]], writes=[self.r_scrb[i % 2]])

            def norm_rest(i):
                qk, g = tiles[i]
                pi = PJ[i % 4]
                p2 = SUMS[i % 2]
                dst = QA if qk == 0 else KA
                rdst = r_qa if qk == 0 else r_ka
                gcol = gq if qk == 0 else gk
                ctx.op("pe", lambda e: e.matmul(self.PS[p2][:], lhsT=self.bd, rhs=self.SCRB[i % 2][:],
                                                start=True, stop=True),
                       reads=[self.r_scrb[i % 2], self.r_const], writes=[self.r_ps[p2]])
                G = self.SCRG[i % 2]
                ctx.op("act", lambda e: e.activation(out=G[:], in_=self.PS[p2][:], func=AF.Ln,
                                                     scale=1.0 / 64, bias=EPS),
                       reads=[self.r_ps[p2]], writes=[self.r_scrg[i % 2]])
                ctx.op("act", lambda e: e.activation(out=G[:], in_=G[:], func=AF.Exp, scale=-0.5),
                       writes=[self.r_scrg[i % 2]])
                for c in range(2):
                    ctx.op("dve", lambda e: e.scalar_tensor_tensor(
                        out=dst[c][0:64, g * 512:(g + 1) * 512], in0=self.PS[pi][c * 64:(c + 1) * 64, :],
                        scalar=gcol[c * 64:(c + 1) * 64, :], in1=G[c * 64:(c + 1) * 64, :],
                        op0=ALU.mult, op1=ALU.mult),
                        reads=[self.r_ps[pi], self.r_scrg[i % 2], self.r_prep, self.r_const], writes=[rdst[c]])

            if h == 0:
                def cb(t):
                    if t >= 4 and t % 2 == 0:
                        i = 2 * (t // 4 - 1) + (t % 4) // 2
                        proj_mm(i)
                        norm_rest(i)
                self.rmsnorm_to_xnt(L, 0, cb=cb, stats_first=True)
                for i in (6, 7):
                    proj_mm(i)
                    norm_rest(i)
            else:
                proj_mm(0)
                for i in range(8):
                    if i + 1 < 8:
                        proj_mm(i + 1)
                    if 1 <= i <= 7:
                        step()
                    norm_rest(i)
            for tg in range(4):
                pi = PJ[tg % 4]

                def mmv(e):
                    last = None
                    for tt in range(4):
                        t = tg * 4 + tt
                        for kc in range(KC):
                            last = e.matmul(self.PS[pi][:, tt * 128:(tt + 1) * 128],
                                            lhsT=self.XNT[:, kc, t * 128:(t + 1) * 128],
                                            rhs=w[:, kc, 256:384], start=(kc == 0), stop=(kc == KC - 1))
                    return last
                ctx.op("pe", mmv, reads=[self.r_slab[bw], self.r_xnt[tg]], writes=[self.r_ps[pi]])
                ctx.op("dve", lambda e: e.tensor_copy(
                    out=VA[:, tg * 4:(tg + 1) * 4, 0:128],
                    in_=self.PS[pi][:].rearrange("p (t c) -> p t c", t=4)),
                    reads=[self.r_ps[pi]], writes=[r_va])
                fill(1)
            self.w_rel(bw)
            for qg in range(4):
                items = [(c, kb) for c in range(2) for kb in range(4 * qg + 4)]

                def emit_S(idx):
                    c, kb = items[idx]
                    jj = kb - 4 * qg
                    col0 = max(jj, 0) * 128
                    pi = SBK[idx % 2]
                    ctx.op("pe", lambda e: e.matmul(
                        self.PS[pi][:, col0:512], lhsT=KA[c][:, kb * 128:(kb + 1) * 128],
                        rhs=QA[c][:, qg * 512 + col0:(qg + 1) * 512], start=True, stop=True),
                        reads=[r_ka[c], r_qa[c]], writes=[self.r_ps[pi]])

                emit_S(0)
                emit_S(1)
                for idx, (c, kb) in enumerate(items):
                    jj = kb - 4 * qg
                    col0 = max(jj, 0) * 128
                    pi = SBK[idx % 2]
                    pb = 2 + idx % 3
                    P = self.SCRB[pb]
                    ctx.op("act", lambda e: e.activation(out=P[:, col0:512], in_=self.PS[pi][:, col0:512],
                                                         func=AF.Exp, bias=-CSHIFT, scale=1.0),
                           reads=[self.r_ps[pi]], writes=[self.r_scrb[pb]])
                    if jj >= 0:
                        ctx.op("dve", lambda e: e.tensor_tensor(out=P[:, col0:col0 + 128], in0=P[:, col0:col0 + 128],
                                                                in1=self.tri, op=ALU.mult),
                               reads=[self.r_const], writes=[self.r_scrb[pb]])
                    tq0 = max(jj, 0)
                    if idx + 2 < len(items):
                        emit_S(idx + 2)

                    def av(e):
                        last = None
                        for tq in range(tq0, 4):
                            last = e.matmul(self.PS[OB[tq]][:, c * 129:(c + 1) * 129],
                                            lhsT=P[:, tq * 128:(tq + 1) * 128], rhs=VA[:, kb, :],
                                            start=(kb == 0), stop=(kb == 4 * qg + tq))
                        return last
                    ctx.op("pe", av, reads=[self.r_scrb[pb], r_va], writes=[self.r_ps[OB[tq]] for tq in range(tq0, 4)])
                    if idx in (1, 2, 3, 4, 7, 10, 13):
                        step()
                    elif jj < -1:
                        fill(1)
                OSB = [self.SCRF[tq] for tq in range(4)]
                r_osb = [self.r_scrf[tq] for tq in range(4)]
                for tq in range(4):
                    ctx.op("dve", lambda e: e.tensor_copy(out=OSB[tq][:, 0:258], in_=self.PS[OB[tq]][:, 0:258]),
                           reads=[self.r_ps[OB[tq]]], writes=[r_osb[tq]])
                st0 = 32
                r_st = self.r_stat[32]
                ssq = self.STAT[:, st0 + 12: st0 + 16]
                r_ss = self.r_stat[33]
                ON = self.SCRB[5]
                r_on = self.r_scrb[5]

                def mk_step1(tq, OSB=OSB, r_osb=r_osb):
                    def f():
                        rden = self.STAT[:, st0 + tq * 2: st0 + tq * 2 + 2]
                        nr1 = self.STAT[:, st0 + 8 + tq: st0 + 9 + tq]
                        ctx.op("dve", lambda e: e.reciprocal(
                            out=rden, in_=OSB[tq][:, 0:258].rearrange("p (c k) -> p c k", c=2)[:, :, 128]),
                            reads=[r_osb[tq]], writes=[r_st])
                        ctx.op("dve", lambda e: e.tensor_tensor(out=nr1, in0=rden[:, 1:2], in1=neglam, op=ALU.mult),
                               reads=[self.r_prep], writes=[r_st])
                        ctx.op("dve", lambda e: e.tensor_scalar(out=OSB[tq][:, 129:257], in0=OSB[tq][:, 129:257],
                                                                scalar1=nr1, scalar2=None, op0=ALU.mult),
                               reads=[r_st], writes=[r_osb[tq]])
                        ctx.op("dve", lambda e: e.scalar_tensor_tensor(
                            out=OSB[tq][:, 0:128], in0=OSB[tq][:, 0:128], scalar=rden[:, 0:1],
                            in1=OSB[tq][:, 129:257], op0=ALU.mult, op1=ALU.add),
                            reads=[r_st], writes=[r_osb[tq]])
                    return f

                def step2(OSB=OSB, r_osb=r_osb):
                    for tq in range(4):
                        ctx.op("act", lambda e: e.activation(out=OSB[tq][:, 258:386], in_=OSB[tq][:, 0:128],
                                                             func=AF.Square, accum_out=ssq[:, tq:tq + 1]),
                               writes=[r_osb[tq], r_ss])
                    ctx.op("act", lambda e: e.activation(out=ssq, in_=ssq, func=AF.Ln, scale=1.0 / 128, bias=EPS),
                           writes=[r_ss])
                    ctx.op("act", lambda e: e.activation(out=ssq, in_=ssq, func=AF.Exp, scale=-0.5),
                           writes=[r_ss])

                def step3(OSB=OSB, r_osb=r_osb):
                    for tq in range(4):
                        ctx.op("dve", lambda e: e.tensor_scalar(out=ON[:, tq * 128:(tq + 1) * 128], in0=OSB[tq][:, 0:128],
                                                                 scalar1=ssq[:, tq:tq + 1], scalar2=None, op0=ALU.mult),
                               reads=[r_osb[tq], r_ss], writes=[r_on])

                def post_pe(qg=qg, ON=ON, r_on=r_on, hb=hb, h=h, bo=bo):
                    drain(h - 2)
                    def tr(e):
                        last = None
                        for tq in range(4):
                            last = e.transpose(self.psb(TRB)[:, tq * 128:(tq + 1) * 128],
                                               ON[:, tq * 128:(tq + 1) * 128], self.ident)
                        return last
                    ctx.op("pe", tr, reads=[r_on, self.r_const], writes=[self.r_ps[TRB]])
                    ctx.op("dve", lambda e: e.tensor_scalar(out=OT[hb][:, qg * 512:(qg + 1) * 512],
                                                            in0=self.psb(TRB)[:, 0:512], scalar1=gsub, scalar2=None,
                                                            op0=ALU.mult),
                           reads=[self.r_ps[TRB], self.r_prep], writes=[r_ot[hb]])
                    if qg == 3:
                        wo = self.SLAB[bo]
                        for t in range(NT):
                            for half in range(2):
                                def wo_fill(t=t, half=half, last=(t == NT - 1 and half == 1)):
                                    wb = WOB + half
                                    ctx.op("pe", lambda e: e.matmul(
                                        self.PS[wb][:], lhsT=OT[hb][:, t * 128:(t + 1) * 128],
                                        rhs=wo[:, half * 512:(half + 1) * 512], start=True, stop=True),
                                        reads=[r_ot[hb], self.r_slab[bo]], writes=[self.r_ps[wb]])
                                    xs = self.X[:, t, half * 512:(half + 1) * 512]
                                    ctx.op("dve", lambda e: e.tensor_tensor(out=xs, in0=self.PS[wb][:], in1=xs,
                                                                            op=ALU.add),
                                           reads=[self.r_ps[wb]], writes=[self.r_x[t]])
                                    if last:
                                        self.w_rel(bo)
                                filler.append((h, wo_fill))
                assert not pending
                pending.extend([mk_step1(0), mk_step1(1), mk_step1(2), mk_step1(3), step2, step3, post_pe])
        flush_post()
        drain(A_HEADS)
        assert not filler

    def retention(self, s, L):
        ctx = self.ctx
        j = L // 2
        A = self.ARENA
        QT = A[:, 0:4096].rearrange("p (k t) -> p k t", k=2)
        KT = A[:, 4096:8192].rearrange("p (k t) -> p k t", k=2)
        SG = A[:, 8192:16384].rearrange("p (n c) -> p n c", n=NT)
        r_qt, r_kt, r_sg = Res("qt"), Res("kt"), Res("sg")
        ggn = self.PFt[:, PF_B + j * 4: PF_B + j * 4 + 4]
        ggnb = ggn.unsqueeze(2).broadcast_to([128, 4, 128])
        TRB = 7
        for h in range(R_HEADS):
            bqk = self.w_get(("rqk", s, L, h))
            bv = self.w_get(("rv", s, L, h))
            bg = self.w_get(("rg", s, L, h))
            bo = self.w_get(("rwo", s, L, h))
            wqk = self.slab3(bqk, 512, 8)
            wv = self.slab3(bv, 512, 8)
            wg = self.slab3(bg, 512, 8)
            wo = self.slab3(bo, 1024, 4)
            dtab = self.CFt[:, CF_DT + h * 128: CF_DT + (h + 1) * 128]
            qdec = self.CFt[:, CF_QDEC + h: CF_QDEC + h + 1]
            kdec = self.CFt[:, CF_KDEC + h: CF_KDEC + h + 1]
            cdec = self.chunk_dec[h]
            def qk_unit(qk, dc, g):
                dst, rd = (QT, r_qt) if qk == 0 else (KT, r_kt)
                if True:
                    if True:
                        pi = self.next_ps(6)

                        def mm(e):
                            last = None
                            for kc in range(KC):
                                last = e.matmul(self.PS[pi][:],
                                                lhsT=wqk[:, kc, qk * 256 + dc * 128: qk * 256 + (dc + 1) * 128],
                                                rhs=self.XNT[:, kc, g * 512:(g + 1) * 512],
                                                start=(kc == 0), stop=(kc == KC - 1))
                            return last
                        ctx.op("pe", mm, reads=[self.r_slab[bqk], self.r_xnt[g]], writes=[self.r_ps[pi]])
                        if (dc * 4 + g) % 2 == 0:
                            ctx.op("act", lambda e: e.activation(out=dst[:, dc, g * 512:(g + 1) * 512],
                                                                 in_=self.PS[pi][:], func=AF.Copy),
                                   reads=[self.r_ps[pi]], writes=[rd])
                        else:
                            ctx.op("dve", lambda e: e.tensor_copy(out=dst[:, dc, g * 512:(g + 1) * 512],
                                                                  in_=self.PS[pi][:]),
                                   reads=[self.r_ps[pi]], writes=[rd])
            units = [(qk, dc) for qk in range(2) for dc in range(2)]
            if h == 0:
                self.rmsnorm_to_xnt(L, 0, cb=lambda t: qk_unit(*units[t % 4], t // 4 - 1) if t >= 4 else None)
                for u in units:
                    qk_unit(*u, 3)
            else:
                for qk, dc in units:
                    for g in range(4):
                        qk_unit(qk, dc, g)
            for n in range(NT):
                pg = self.next_ps(7)

                def mmg(e):
                    last = None
                    for kc in range(KC):
                        last = e.matmul(self.PS[pg][:], lhsT=self.XNT[:, kc, n * 128:(n + 1) * 128], rhs=wg[:, kc, :],
                                        start=(kc == 0), stop=(kc == KC - 1))
                    return last
                ctx.op("pe", mmg, reads=[self.r_slab[bg], self.r_xnt[n // 4]], writes=[self.r_ps[pg]])
                ctx.op("act", lambda e: e.activation(out=SG[:, n, :], in_=self.PS[pg][:], func=AF.Silu),
                       reads=[self.r_ps[pg]], writes=[r_sg])
            self.w_rel(bg)

            def stage_a(n):
                tok = slice(n * 128, (n + 1) * 128)
                pv = self.next_ps(7)

                def mmv(e):
                    last = None
                    for kc in range(KC):
                        last = e.matmul(self.PS[pv][:], lhsT=self.XNT[:, kc, tok], rhs=wv[:, kc, :],
                                        start=(kc == 0), stop=(kc == KC - 1))
                    return last
                ctx.op("pe", mmv, reads=[self.r_slab[bv], self.r_xnt[n // 4]], writes=[self.r_ps[pv]])
                vb = n % 2
                ctx.op("act", lambda e: e.activation(out=self.SCRB[vb][:], in_=self.PS[pv][:], func=AF.Copy),
                       reads=[self.r_ps[pv]], writes=[self.r_scrb[vb]])
                if n < NT - 1:
                    KD = self.SCRB[2 + n % 2]
                    pk = self.next_ps(7)

                    def mmk(e):
                        last = None
                        for kc in range(KC):
                            last = e.matmul(self.PS[pk][:, 0:256], lhsT=self.XNT[:, kc, tok],
                                            rhs=wqk[:, kc, 256:512], start=(kc == 0), stop=(kc == KC - 1))
                        return last
                    ctx.op("pe", mmk, reads=[self.r_slab[bqk], self.r_xnt[n // 4]], writes=[self.r_ps[pk]])
                    ctx.op("act", lambda e: e.activation(out=KD[:, 0:256], in_=self.PS[pk][:, 0:256], func=AF.Copy,
                                                         scale=kdec),
                           reads=[self.r_ps[pk], self.r_const], writes=[self.r_scrb[2 + n % 2]])
                psc = self.next_ps(7)

                def mms(e):
                    last = None
                    for dc in range(2):
                        last = e.matmul(self.PS[psc][:, 0:128], lhsT=KT[:, dc, tok], rhs=QT[:, dc, tok],
                                        start=(dc == 0), stop=(dc == 1))
                    return last
                ctx.op("pe", mms, reads=[r_kt, r_qt], writes=[self.r_ps[psc]])
                AM = self.SCRB[4 + n % 2]
                ctx.op("dve", lambda e: e.tensor_tensor(out=AM[:, 0:128], in0=self.PS[psc][:, 0:128], in1=dtab,
                                                        op=ALU.mult),
                       reads=[self.r_ps[psc], self.r_const], writes=[self.r_scrb[4 + n % 2]])

            ybufs = {}

            def stage_b(n):
                tok = slice(n * 128, (n + 1) * 128)
                vb = n % 2
                V = self.SCRB[vb]
                KD = self.SCRB[2 + n % 2]
                AM = self.SCRB[4 + n % 2]
                pacc = self.next_ps(7)

                def mma(e):
                    last = e.matmul(self.PS[pacc][:], lhsT=AM[:, 0:128], rhs=V[:], start=True, stop=(n == 0))
                    if n > 0:
                        for dc in range(2):
                            last = e.matmul(self.PS[pacc][:], lhsT=QT[:, dc, tok], rhs=self.RB[:, dc, :],
                                            start=False, stop=(dc == 1))
                    return last
                ctx.op("pe", mma, reads=[self.r_scrb[4 + n % 2], self.r_scrb[vb], r_qt] +
                       ([self.r_rb] if n > 0 else []), writes=[self.r_ps[pacc]])
                st = 48 + (n % 2) * 4
                ssq = self.STAT[:, st:st + 1]
                sc2 = self.STAT[:, st + 1:st + 2]
                r_st = self.r_stat[48 + n % 2]
                ctx.op("act", lambda e: e.activation(out=self.JUNK[:, 0:512], in_=self.PS[pacc][:], func=AF.Square,
                                                     scale=qdec, accum_out=ssq),
                       reads=[self.r_ps[pacc], self.r_const], writes=[self.r_junk, r_st])
                ctx.op("act", lambda e: e.activation(out=ssq, in_=ssq, func=AF.Sqrt, scale=1.0 / 512, bias=EPS),
                       writes=[r_st])
                if n < NT - 1:
                    for dc in range(2):
                        pr = self.next_ps(7)
                        ctx.op("pe", lambda e: e.matmul(self.PS[pr][:], lhsT=KD[:, dc * 128:(dc + 1) * 128], rhs=V[:],
                                                        start=True, stop=True),
                               reads=[self.r_scrb[2 + n % 2], self.r_scrb[vb]], writes=[self.r_ps[pr]])
                        if n == 0:
                            ctx.op("dve", lambda e: e.tensor_copy(out=self.R32[:, dc, :], in_=self.PS[pr][:]),
                                   reads=[self.r_ps[pr]], writes=[self.r_r32])
                        else:
                            ctx.op("dve", lambda e: e.scalar_tensor_tensor(
                                out=self.R32[:, dc, :], in0=self.R32[:, dc, :], scalar=cdec, in1=self.PS[pr][:],
                                op0=ALU.mult, op1=ALU.add),
                                reads=[self.r_ps[pr]], writes=[self.r_r32])
                ctx.op("dve", lambda e: e.reciprocal(out=ssq, in_=ssq), writes=[r_st])
                ctx.op("dve", lambda e: e.tensor_scalar(out=sc2, in0=ssq, scalar1=qdec, scalar2=None, op0=ALU.mult),
                       reads=[self.r_const], writes=[r_st])
                YB = self.SCRF[2 + n % 2]
                r_yb = self.r_scrf[2 + n % 2]
                Y = YB[:].bitcast(BF16)[:, 0:512]
                ctx.op("dve", lambda e: e.scalar_tensor_tensor(out=Y, in0=self.PS[pacc][:], scalar=sc2,
                                                               in1=SG[:, n, :], op0=ALU.mult, op1=ALU.mult),
                       reads=[self.r_ps[pacc], r_sg, r_st], writes=[r_yb])

                if n < NT - 1:
                    ctx.op("act", lambda e: e.activation(out=self.RB[:, 0, :], in_=self.R32[:, 0, :], func=AF.Copy),
                           reads=[self.r_r32], writes=[self.r_rb])
                    ctx.op("act", lambda e: e.activation(out=self.RB[:, 1, :], in_=self.R32[:, 1, :], func=AF.Copy),
                           reads=[self.r_r32], writes=[self.r_rb])

            def stage_c(n):
                YB = self.SCRF[2 + n % 2]
                r_yb = self.r_scrf[2 + n % 2]
                Y = YB[:].bitcast(BF16)[:, 0:512]
                YT = YB[:].bitcast(BF16)[:, 512:1024]

                def tr(e):
                    last = None
                    for ec in range(4):
                        last = e.transpose(self.psb(TRB)[:, ec * 128:(ec + 1) * 128], Y[:, ec * 128:(ec + 1) * 128],
                                           self.ident)
                    return last
                ctx.op("pe", tr, reads=[r_yb, self.r_const], writes=[self.r_ps[TRB]])
                ctx.op("dve", lambda e: e.tensor_tensor(
                    out=YT.rearrange("p (k c) -> p k c", k=4),
                    in0=self.psb(TRB)[:, 0:512].rearrange("p (k c) -> p k c", k=4), in1=ggnb, op=ALU.mult),
                    reads=[self.r_ps[TRB], self.r_const], writes=[r_yb])
                for half in range(2):
                    po = self.next_ps(7)

                    def mmo(e):
                        last = None
                        for ec in range(4):
                            last = e.matmul(self.PS[po][:], lhsT=YT[:, ec * 128:(ec + 1) * 128],
                                            rhs=wo[:, ec, half * 512:(half + 1) * 512],
                                            start=(ec == 0), stop=(ec == 3))
                        return last
                    ctx.op("pe", mmo, reads=[r_yb, self.r_slab[bo]], writes=[self.r_ps[po]])
                    xs = self.X[:, n, half * 512:(half + 1) * 512]
                    ctx.op("dve", lambda e: e.tensor_tensor(out=xs, in0=self.PS[po][:], in1=xs, op=ALU.add),
                           reads=[self.r_ps[po]], writes=[self.r_x[n]])

            stage_a(0)
            for n in range(NT):
                if n + 1 < NT:
                    stage_a(n + 1)
                if n >= 1:
                    stage_c(n - 1)
                stage_b(n)
            stage_c(NT - 1)
            for b in (bqk, bv, bo):
                self.w_rel(b)

    def prologue(self):
        ctx = self.ctx
        ctx.dma("sp", [(self.PFt[:], self.d_pf), (self.CFt[:], self.d_cf), (self.CBt[:], self.d_cb)],
                self.s_c, writes=[self.r_const])
        for j in range(2):
            L = 2 * j
            lam_init = 0.8 - 0.6 * math.exp(-0.3 * L)
            pa = PF_A + j * PF_A_STRIDE
            ctx.op("dve", lambda e: e.tensor_scalar(out=self.PREP[:, 2 * j:2 * j + 1], in0=self.PFt[:, pa:pa + 1],
                                                    scalar1=0.125, scalar2=None, op0=ALU.mult),
                   reads=[self.r_const], writes=[self.r_prep])
            ctx.op("dve", lambda e: e.tensor_scalar(out=self.PREP[:, 4 + j:5 + j], in0=self.PFt[:, pa + 2:pa + 3],
                                                    scalar1=1.0 - lam_init, scalar2=None, op0=ALU.mult),
                   reads=[self.r_const], writes=[self.r_prep])
            lam = self.PFt[:, pa + 3:pa + 259]
            r_l = Res("lam")
            for i in range(2):
                ctx.op("dve", lambda e: e.tensor_tensor(out=self.LAMT[:, i * 64:(i + 1) * 64],
                                                        in0=lam[:, i * 128:i * 128 + 64],
                                                        in1=lam[:, i * 128 + 64:i * 128 + 128], op=ALU.mult),
                       reads=[self.r_const], writes=[r_l])
                ctx.op("act", lambda e: e.activation(out=self.JUNK[:, 0:64], in_=self.LAMT[:, i * 64:(i + 1) * 64],
                                                     func=AF.Copy, accum_out=self.PREP[:, 12 + i:13 + i]),
                       reads=[r_l], writes=[self.r_junk, self.r_prep])
                ctx.op("act", lambda e: e.activation(out=self.PREP[:, 12 + i:13 + i], in_=self.PREP[:, 12 + i:13 + i],
                                                     func=AF.Exp),
                       writes=[self.r_prep])
            ctx.op("dve", lambda e: e.tensor_tensor(out=self.PREP[:, 8 + j:9 + j], in0=self.PREP[:, 13:14],
                                                    in1=self.PREP[:, 12:13], op=ALU.subtract),
                   writes=[self.r_prep])
            ctx.op("dve", lambda e: e.tensor_scalar(out=self.PREP[:, 8 + j:9 + j], in0=self.PREP[:, 8 + j:9 + j],
                                                    scalar1=-lam_init, scalar2=None, op0=ALU.add),
                   writes=[self.r_prep])

    def issue_p(self, s, L):
        PBF = self.ARENA[:, 0:4096].rearrange("p (t c) -> p t c", t=NT)
        self.ctx.dma("pool", [(PBF, self.d_p[L, s].rearrange("(t p) c -> p t c", p=128))], self.s_p,
                     writes=[self.r_ht[0], self.r_pbf])
        self.p_issued = (s, L)

    def x_load(self, s, t):
        self.ctx.dma("sp", [(self.X[:, t, :], self.d_x[s, t * 128:(t + 1) * 128, :])], self.s_xl[t],
                     writes=[self.r_x[t]])

    def x_store(self, s, t):
        self.ctx.dma("sp", [(self.d_y[s, t * 128:(t + 1) * 128, :], self.X[:, t, :])], self.s_xs[t],
                     reads=[self.r_x[t]])

    def build(self):
        nc = self.nc
        self._dram()
        with ExitStack() as es:
            self._alloc(es)
            self.ctx = Ctx(nc, es, self.sig)
            ctx = self.ctx
            self.w_init()
            self.prologue()
            for t in range(NT):
                self.x_load(0, t)
            for s in range(self.nseq):
                for L in self.layers:
                    if L % 2 == 0:
                        self.attention(s, L)
                    else:
                        self.retention(s, L)
                    ctx.barrier()
                    self.mlp(s, L)
                    ctx.barrier()
                    self.ple(s, L, final=(L == self.layers[-1]))
                    ctx.barrier()
            for t in range(NT):
                nc.sync.wait_ge(self.s_xs[t].h, self.s_xs[t].n)
            assert self.w_next_get == len(self.plan)
        return nc


_WEIGHT_KEYS = ("a_w_qkv", "a_w_o", "r_w_in", "r_w_out", "mlp_w1", "mlp_w2", "pe_w_up", "pe_w_gate")


def _run(inputs, x_full, layers, core_ids=None):
    cb, cf, augb, _ = _const_tables()
    pf = _pack_params(inputs)
    dry = Builder(layers)
    dry.build()
    nc = Builder(layers, sig=dry.ctx.log).build()
    n = N_CORES if core_ids is None else len(core_ids)
    w = {k: np.ascontiguousarray(np.asarray(inputs[k], np.float32)) for k in _WEIGHT_KEYS}
    p = np.asarray(inputs["p"], np.float32)
    in_maps = []
    for c in range(n):
        b0 = c * SEQ_PER_CORE
        m = dict(w)
        m["x"] = np.ascontiguousarray(x_full[b0:b0 + SEQ_PER_CORE])
        m["p"] = np.ascontiguousarray(p[:, b0:b0 + SEQ_PER_CORE])
        m["params_f32"] = pf
        m["consts_f32"] = cf
        m["consts_bf16"] = cb
        m["aug_bf16"] = augb
        in_maps.append(m)
    res = run_bass_kernel_spmd(nc, in_maps, core_ids=list(range(n)), **({"trace": True} if TRACE else {}))
    if TRACE:
        print("exec_time_ns", res.exec_time_ns)
    return np.concatenate([np.asarray(r["y"], np.float32) for r in res.results], axis=0)


FUSED = True
TRACE = False


def kernel(**inputs):
    x = np.ascontiguousarray(np.asarray(inputs["x"], np.float32))
    if FUSED:
        return _run(inputs, x, [0, 1, 2, 3])
    for L in range(DEPTH):
        x = _run(inputs, x, [L])
    return x
```
